# Optimizing a Trainium2 kernel written in Bass

```python
import math
import jax, jax.numpy as jnp
from jax import lax
import numpy as np

D_MODEL = 1024
BATCH = 2
SEQ = 8192
DEPTH = 4
DEC_BATCH = 32
DEC_SEQ = 8
PAST_LEN = 8192
PAGE_SIZE = 128

N_AB_LAYERS = (DEPTH + 1) // 2
N_C_LAYERS = DEPTH // 2
CHUNK = 128
W_A = D_MODEL // 2
G_A = 4
GA_CH = W_A // G_A
W_B = D_MODEL // 2
CONV_W = 3
WINDOWS = (128, 512, 2048)
DILATIONS = (1, 4, 16)
N_GROUPS = 3
H_G = 8
HEAD_DIM = 64
C_WIDTH = H_G * HEAD_DIM
QKV_COLS = N_GROUPS * C_WIDTH
D_FF = 4 * D_MODEL
ALPHA = (2.0 * DEPTH) ** 0.25
BETA = (8.0 * DEPTH) ** -0.25
LN_EPS = 1e-5
NEG_INF = -1e30

kernel_name = 'hybrid_gmlp_shortconv_dilated_attn_step'


def layer_norm(x, g, b):
    x32 = x.astype(jnp.float32)
    mu = jnp.mean(x32, -1, keepdims=True)
    var = jnp.mean(jnp.square(x32 - mu), -1, keepdims=True)
    y = (x32 - mu) * lax.rsqrt(var + LN_EPS)
    return (y * g.astype(jnp.float32) + b.astype(jnp.float32)).astype(x.dtype)


def alibi_slopes():
    return 2.0 ** (-(8.0 / H_G) * jnp.arange(1, H_G + 1, dtype=jnp.float32))


def chunk_spatial_mix(v, w_s, b_s):
    B, L, _ = v.shape
    nc = -(-L // CHUNK)
    Lp = nc * CHUNK
    vp = jnp.pad(v, ((0, 0), (0, Lp - L), (0, 0))).reshape(B, nc, CHUNK, G_A, GA_CH)
    causal = jnp.tril(jnp.ones((CHUNK, CHUNK), dtype=bool))
    w = jnp.where(causal[None], w_s, 0.0).astype(v.dtype)
    y = jnp.einsum('gij,bnjgc->bnigc', w, vp) + b_s.T.astype(v.dtype)[None, None, :, :, None]
    return y.reshape(B, Lp, W_A)[:, :L]


def causal_conv(buf, xin, w):
    L = xin.shape[1]
    xc = jnp.concatenate([buf, xin], axis=1)
    y = sum(xc[:, k:k + L] * w[k] for k in range(CONV_W))
    return y, xc[:, -(CONV_W - 1):]


def ab_mixer(x, conv_buf, w_in, ln_v_g, ln_v_b, w_s, b_s, conv_w, w_out):
    z = jnp.einsum('bld,de->ble', x, w_in)
    z_u, z_v, z_bg, z_cg, z_h = jnp.split(
        z, [W_A, 2 * W_A, 2 * W_A + W_B, 2 * W_A + 2 * W_B], axis=-1)
    u = jax.nn.gelu(z_u)
    v = layer_norm(jax.nn.gelu(z_v), ln_v_g, ln_v_b)
    a_out = u * chunk_spatial_mix(v, w_s, b_s)
    conv_out, new_buf = causal_conv(conv_buf, z_cg * z_h, conv_w)
    b_out = z_bg * conv_out
    out = jnp.einsum('ble,ed->bld', jnp.concatenate([a_out, b_out], axis=-1), w_out)
    return out, new_buf, v


def split_qkv(x, w_qkv):
    B, L, _ = x.shape
    qkv = jnp.einsum('bld,de->ble', x, w_qkv).reshape(B, L, 3, N_GROUPS, H_G, HEAD_DIM)
    return qkv[:, :, 0], qkv[:, :, 1], qkv[:, :, 2]


def dilated_group_prompt(q, k, v, d, n_back, slopes):
    B, S, H, dh = q.shape
    M = S // d
    nb = -(-M // CHUNK)
    Mp = nb * CHUNK

    def by_residue(a):
        a = a.reshape(B, M, d, H, dh).transpose(0, 2, 1, 3, 4)
        return jnp.pad(a, ((0, 0), (0, 0), (0, Mp - M), (0, 0), (0, 0)))

    def key_blocks(a):
        a = jnp.pad(by_residue(a), ((0, 0), (0, 0), (CHUNK, 0), (0, 0), (0, 0)))
        a = a.reshape(B, d, nb + 1, CHUNK, H, dh)
        return jnp.concatenate([a[:, :, :-1], a[:, :, 1:]], axis=3)

    qb = by_residue(q).reshape(B, d, nb, CHUNK, H, dh)
    kb, vb = key_blocks(k), key_blocks(v)
    logits = jnp.einsum('brnqhc,brnkhc->brnhqk', qb, kb).astype(jnp.float32) * (dh ** -0.5)
    qi = jnp.arange(CHUNK)[:, None]
    kj = jnp.arange(2 * CHUNK)[None, :]
    steps = CHUNK + qi - kj
    key_m = jnp.arange(nb)[:, None] * CHUNK + jnp.arange(2 * CHUNK)[None, :] - CHUNK
    valid = ((steps >= 0) & (steps <= n_back))[None] & (key_m >= 0)[:, None, :]
    bias = -slopes[:, None, None] * (steps * d).astype(jnp.float32)[None]
    logits = jnp.where(valid[:, None], logits + bias, NEG_INF)
    lse = jax.nn.logsumexp(logits, axis=-1)
    p = jnp.exp(logits - lse[..., None]).astype(v.dtype)
    o = jnp.einsum('brnhqk,brnkhc->brnqhc', p, vb)
    o = o.reshape(B, d, Mp, H, dh)[:, :, :M].transpose(0, 2, 1, 3, 4).reshape(B, S, H, dh)
    lse = lse.transpose(0, 1, 2, 4, 3).reshape(B, d, Mp, H)[:, :, :M]
    lse = lse.transpose(0, 2, 1, 3).reshape(B, S, H)
    return o, lse


def dilated_group_sample(q, k, v, k_buf, v_buf, d, n_back, slopes):
    T, dh = q.shape[1], q.shape[-1]
    L = k_buf.shape[1]
    kc = jnp.concatenate([k_buf, k], axis=1)
    vc = jnp.concatenate([v_buf, v], axis=1)
    steps = jnp.arange(n_back + 1)[None, :]
    idx = L + jnp.arange(T)[:, None] - steps * d
    valid = idx >= 0
    idx = jnp.maximum(idx, 0)
    kg = jnp.take(kc, idx, axis=1)
    vg = jnp.take(vc, idx, axis=1)
    logits = jnp.einsum('bthc,btkhc->bhtk', q, kg).astype(jnp.float32) * (dh ** -0.5)
    bias = -slopes[:, None, None] * (steps * d).astype(jnp.float32)[None]
    logits = jnp.where(valid[None, None], logits + bias, NEG_INF)
    lse = jax.nn.logsumexp(logits, axis=-1)
    p = jnp.exp(logits - lse[..., None]).astype(v.dtype)
    o = jnp.einsum('bhtk,btkhc->bthc', p, vg)
    return o, lse.transpose(0, 2, 1), kc[:, -L:], vc[:, -L:]


def merge_groups(outs, lses, w_out):
    alpha = jax.nn.softmax(jnp.stack(lses, axis=0), axis=0)
    o = sum(alpha[g][..., None].astype(outs[g].dtype) * outs[g] for g in range(N_GROUPS))
    B, L = o.shape[:2]
    return jnp.einsum('ble,ed->bld', o.reshape(B, L, C_WIDTH), w_out)


def c_mixer_prompt(x, w_qkv, w_out):
    q, k, v = split_qkv(x, w_qkv)
    slopes = alibi_slopes()
    S = x.shape[1]
    outs, lses, new_kv = [], [], []
    for g in range(N_GROUPS):
        d = DILATIONS[g]
        o, l = dilated_group_prompt(q[:, :, g], k[:, :, g], v[:, :, g], d, WINDOWS[g] // d, slopes)
        outs.append(o)
        lses.append(l)
        n_keep = min(WINDOWS[g], S)
        new_kv.append(jnp.stack([k[:, -n_keep:, g], v[:, -n_keep:, g]], axis=2))
    return merge_groups(outs, lses, w_out), new_kv


def c_mixer_sample(x, bufs, w_qkv, w_out):
    q, k, v = split_qkv(x, w_qkv)
    slopes = alibi_slopes()
    outs, lses, new_kv = [], [], []
    for g in range(N_GROUPS):
        d = DILATIONS[g]
        o, l, kb, vb = dilated_group_sample(q[:, :, g], k[:, :, g], v[:, :, g],
                                            bufs[g][:, :, 0], bufs[g][:, :, 1],
                                            d, WINDOWS[g] // d, slopes)
        outs.append(o)
        lses.append(l)
        new_kv.append(jnp.stack([kb, vb], axis=2))
    return merge_groups(outs, lses, w_out), new_kv


def sq_relu_mlp(x, w_up, w_down):
    h = jnp.square(jax.nn.relu(jnp.einsum('bld,df->blf', x, w_up)))
    return jnp.einsum('blf,fd->bld', h, w_down)


def setup_inputs(seed: int = 0) -> dict:
    key = jax.random.key(seed)
    ks = jax.random.split(key, 24)
    f32 = jnp.float32

    def nrm(k, shape, scale):
        return jax.random.normal(k, shape, f32) * scale

    win_lens = [min(w, PAST_LEN) for w in WINDOWS]
    col_scale = jnp.concatenate([jnp.ones((2 * QKV_COLS,), f32), jnp.full((QKV_COLS,), BETA, f32)])
    return {
        'x_prompt': nrm(ks[0], (BATCH, SEQ, D_MODEL), 1.0),
        'x_sample': nrm(ks[1], (DEC_BATCH, DEC_SEQ, D_MODEL), 1.0),
        'state_conv': nrm(ks[2], (N_AB_LAYERS, DEC_BATCH, CONV_W - 1, W_B), 1.0),
        'cache_kv_w128': nrm(ks[3], (N_C_LAYERS, DEC_BATCH, win_lens[0], 2, H_G, HEAD_DIM), 1.0),
        'cache_kv_w512': nrm(ks[4], (N_C_LAYERS, DEC_BATCH, win_lens[1], 2, H_G, HEAD_DIM), 1.0),
        'cache_kv_w2048': nrm(ks[5], (N_C_LAYERS, DEC_BATCH, win_lens[2], 2, H_G, HEAD_DIM), 1.0),
        'w_in_ab': nrm(ks[6], (N_AB_LAYERS, D_MODEL, 2 * W_A + 3 * W_B), D_MODEL ** -0.5),
        'ln_v_g': 1.0 + nrm(ks[7], (N_AB_LAYERS, W_A), 0.05),
        'ln_v_b': nrm(ks[8], (N_AB_LAYERS, W_A), 0.05),
        'w_spatial': nrm(ks[9], (N_AB_LAYERS, G_A, CHUNK, CHUNK), CHUNK ** -0.5),
        'b_spatial': 1.0 + nrm(ks[10], (N_AB_LAYERS, G_A, CHUNK), 0.1),
        'conv_w': nrm(ks[11], (N_AB_LAYERS, CONV_W, W_B), CONV_W ** -0.5),
        'w_out_ab': nrm(ks[12], (N_AB_LAYERS, W_A + W_B, D_MODEL), BETA * (W_A + W_B) ** -0.5),
        'w_qkv_c': nrm(ks[13], (N_C_LAYERS, D_MODEL, 3 * QKV_COLS), D_MODEL ** -0.5) * col_scale,
        'w_out_c': nrm(ks[14], (N_C_LAYERS, C_WIDTH, D_MODEL), BETA * C_WIDTH ** -0.5),
        'ln1_g': 1.0 + nrm(ks[15], (DEPTH, D_MODEL), 0.05),
        'ln1_b': nrm(ks[16], (DEPTH, D_MODEL), 0.05),
        'ln2_g': 1.0 + nrm(ks[17], (DEPTH, D_MODEL), 0.05),
        'ln2_b': nrm(ks[18], (DEPTH, D_MODEL), 0.05),
        'w_mlp_up': nrm(ks[19], (DEPTH, D_MODEL, D_FF), BETA * D_MODEL ** -0.5),
        'w_mlp_down': nrm(ks[20], (DEPTH, D_FF, D_MODEL), BETA * D_FF ** -0.5),
    }


def reference(x_prompt, x_sample, state_conv, cache_kv_w128, cache_kv_w512, cache_kv_w2048,
              w_in_ab, ln_v_g, ln_v_b, w_spatial, b_spatial, conv_w, w_out_ab,
              w_qkv_c, w_out_c, ln1_g, ln1_b, ln2_g, ln2_b, w_mlp_up, w_mlp_down):
    caches = (cache_kv_w128, cache_kv_w512, cache_kv_w2048)
    xp, xs = x_prompt, x_sample
    conv_p, conv_s, chunk_v_s = [], [], []
    kv_p = [[] for _ in range(N_GROUPS)]
    kv_s = [[] for _ in range(N_GROUPS)]
    for layer in range(DEPTH):
        i = layer // 2
        if layer % 2 == 0:
            params = (w_in_ab[i], ln_v_g[i], ln_v_b[i], w_spatial[i], b_spatial[i], conv_w[i], w_out_ab[i])
            zero_buf = jnp.zeros((xp.shape[0], CONV_W - 1, W_B), xp.dtype)
            mp, buf_p, _ = ab_mixer(xp, zero_buf, *params)
            ms, buf_s, v_s = ab_mixer(xs, state_conv[i], *params)
            conv_p.append(buf_p)
            conv_s.append(buf_s)
            chunk_v_s.append(v_s)
        else:
            mp, new_p = c_mixer_prompt(xp, w_qkv_c[i], w_out_c[i])
            ms, new_s = c_mixer_sample(xs, [c[i] for c in caches], w_qkv_c[i], w_out_c[i])
            for g in range(N_GROUPS):
                kv_p[g].append(new_p[g])
                kv_s[g].append(new_s[g])
        xp = layer_norm(ALPHA * xp + mp, ln1_g[layer], ln1_b[layer])
        xs = layer_norm(ALPHA * xs + ms, ln1_g[layer], ln1_b[layer])
        xp = layer_norm(ALPHA * xp + sq_relu_mlp(xp, w_mlp_up[layer], w_mlp_down[layer]), ln2_g[layer], ln2_b[layer])
        xs = layer_norm(ALPHA * xs + sq_relu_mlp(xs, w_mlp_up[layer], w_mlp_down[layer]), ln2_g[layer], ln2_b[layer])
    return (xp, xs,
            jnp.stack(conv_p), jnp.stack(conv_s), jnp.stack(chunk_v_s),
            jnp.stack(kv_p[0]), jnp.stack(kv_p[1]), jnp.stack(kv_p[2]),
            jnp.stack(kv_s[0]), jnp.stack(kv_s[1]), jnp.stack(kv_s[2]))
```

```python
import numpy as np
from contextlib import ExitStack
import concourse.bass as bass
import concourse.mybir as mybir
from concourse.bass_utils import run_bass_kernel_spmd

F32 = mybir.dt.float32
BF16 = mybir.dt.bfloat16
ALU = mybir.AluOpType
AF = mybir.ActivationFunctionType

D = 1024
SEQ = 8192
NT = 2048
NSG = SEQ // NT
TS = 32
ALPHA = (2.0 * 4) ** 0.25
EPS = 1e-5
BIG = 1.0e30
WIN = (128, 512, 2048)
DIL = (1, 4, 16)
SLOPES = [2.0 ** (-(h + 1)) for h in range(8)]
NCORES = 8
EPOCH = 30000
DBG = set()


class Sched:
    ENG = ['pe', 'act', 'dve', 'pool', 'sp']

    def __init__(self, nc, es, n_dma_sems=40):
        self.nc = nc
        self.es = es
        self.ops = {e: [] for e in self.ENG}
        self.cnt = {e: 0 for e in self.ENG}
        self.epoch = {e: 0 for e in self.ENG}
        self.known = {e: {} for e in self.ENG}
        self.res_w = {}
        self.res_r = {}
        self.sems = {}
        for e in self.ENG:
            self.sems[('e', e, 0)] = es.enter_context(nc.semaphore('s_%s0' % e))
        self.dsem = [es.enter_context(nc.semaphore('d%d' % i)) for i in range(n_dma_sems)]
        self.dcnt = [0] * n_dma_sems
        half = n_dma_sems // 2
        self.dring = {'pool': list(range(0, half)), 'sp': list(range(half, n_dma_sems))}
        self.dnext = {'pool': 0, 'sp': 0}

    def _semof(self, ev):
        if ev[0] == 'd':
            return self.dsem[ev[1]]
        return self.sems[ev[0:3]]

    def _wait(self, engine, ev):
        key = ev[0:3] if ev[0] == 'e' else ev[0:2]
        val = ev[-1]
        if self.known[engine].get(key, 0) >= val:
            return
        self.known[engine][key] = val
        sem = self._semof(ev)
        self.ops[engine].append(lambda eng, sem=sem, val=val: eng.wait_ge(sem, val))

    def _deps(self, engine, reads, writes):
        deps = []
        for r in reads:
            if r in self.res_w:
                deps.append(self.res_w[r])
        for w in writes:
            if w in self.res_w:
                deps.append(self.res_w[w])
            deps.extend(self.res_r.get(w, []))
        for ev in deps:
            if ev[0] == 'e' and ev[1] == engine and engine == 'pe':
                continue
            self._wait(engine, ev)

    def _record(self, ev, reads, writes):
        for r in reads:
            lst = self.res_r.setdefault(r, [])
            if ev[0] == 'e':
                lst[:] = [x for x in lst if not (x[0] == 'e' and x[1] == ev[1] and x[2] == ev[2])]
            lst.append(ev)
        for w in writes:
            self.res_w[w] = ev
            self.res_r[w] = []

    def op(self, engine, fn, reads=(), writes=()):
        self._deps(engine, reads, writes)
        if self.cnt[engine] >= EPOCH:
            self.epoch[engine] += 1
            self.cnt[engine] = 0
            self.sems[('e', engine, self.epoch[engine])] = self.es.enter_context(
                self.nc.semaphore('s_%s%d' % (engine, self.epoch[engine])))
        self.cnt[engine] += 1
        ev = ('e', engine, self.epoch[engine], self.cnt[engine])
        sem = self.sems[ev[0:3]]
        self.ops[engine].append(lambda eng, fn=fn, sem=sem: fn(eng).then_inc(sem, 1))
        self._record(ev, reads, writes)
        return ev

    def dma(self, queue, fn, reads=(), writes=()):
        self._deps(queue, reads, writes)
        ring = self.dring[queue]
        j = ring[self.dnext[queue] % len(ring)]
        self.dnext[queue] += 1
        if self.dcnt[j] > 0:
            self._wait(queue, ('d', j, 16 * self.dcnt[j]))
        self.dcnt[j] += 1
        ev = ('d', j, 16 * self.dcnt[j])
        sem = self.dsem[j]
        self.ops[queue].append(lambda eng, fn=fn, sem=sem: fn(eng).then_inc(sem, 16))
        self._record(ev, reads, writes)
        return ev

    def finish(self):
        for j, c in enumerate(self.dcnt):
            if c > 0:
                self._wait('sp', ('d', j, 16 * c))
        for e in ['pe', 'act', 'dve', 'pool']:
            if self.cnt[e] > 0:
                self._wait('sp', ('e', e, self.epoch[e], self.cnt[e]))

    def emit(self):
        nc = self.nc
        ops = self.ops
        with nc.Block() as block:
            @block.tensor
            def _(e):
                for f in ops['pe']:
                    f(e)

            @block.scalar
            def _(e):
                for f in ops['act']:
                    f(e)

            @block.vector
            def _(e):
                for f in ops['dve']:
                    f(e)

            @block.gpsimd
            def _(e):
                for f in ops['pool']:
                    f(e)

            @block.sync
            def _(e):
                for f in ops['sp']:
                    f(e)


def build_program(nsg=NSG, do_sample=True, do_prompt=True, nlayers=4, copy_cache=True):
    nc = bass.Bass("TRN2", target_bir_lowering=False)

    def din(name, shape, dt=F32):
        return nc.dram_tensor(name, list(shape), dt, kind="ExternalInput").ap()

    def dout(name, shape):
        return nc.dram_tensor(name, list(shape), F32, kind="ExternalOutput").ap()

    xp = din("xp", [SEQ, D])
    xs = din("xs", [TS, D])
    sconv = din("sconv", [2, 4, 2, 512])
    cks = [din("ck128", [2, 4, 128, 1024]), din("ck512", [2, 4, 512, 1024]), din("ck2048", [2, 4, 2048, 1024])]
    w_in = din("w_in_ab", [2, D, 2560])
    lnv_g = din("ln_v_g", [2, 512])
    lnv_b = din("ln_v_b", [2, 512])
    w_sp = din("w_spatial", [2, 4, 128, 128])
    b_sp = din("b_spatial", [2, 4, 128])
    conv_w = din("conv_w", [2, 3, 512])
    w_out_ab = din("w_out_ab", [2, D, D])
    w_qkv = din("w_qkv_c", [2, D, 4608])
    w_out_c = din("w_out_c", [2, 512, D])
    ln1g = din("ln1_g", [4, D])
    ln1b = din("ln1_b", [4, D])
    ln2g = din("ln2_g", [4, D])
    ln2b = din("ln2_b", [4, D])
    w_up = din("w_mlp_up", [4, D, 4096])
    w_dn = din("w_mlp_down", [4, 4096, D])
    c_ident = din("c_ident", [128, 128])
    c_steps = din("c_steps", [128, 256])
    c_tril = din("c_tril", [128, 128])
    c_bd32 = din("c_bd32", [32, 32])
    c_sbias = din("c_sbias", [128, 24 * 8])
    c_nsteps = din("c_nsteps", [32, 3 * 32])
    c_hmask = din("c_hmask", [8, 520])
    c_sel = din("c_sel", [8, 32 * 32])
    c_sel65 = din("c_sel65", [65, 64])

    yp = dout("yp", [SEQ, D])
    ys = dout("ys", [TS, D])
    o_convp = dout("o_convp", [2, 2, 512])
    o_convs = dout("o_convs", [2, 4, 2, 512])
    o_chunkv = dout("o_chunkv", [2, 4, 8, 512])
    o_kvp = [dout("o_kvp128", [2, 128, 1024]), dout("o_kvp512", [2, 512, 1024]), dout("o_kvp2048", [2, 2048, 1024])]
    o_kvs = [dout("o_kvs128", [2, 4, 128, 1024]), dout("o_kvs512", [2, 4, 512, 1024]),
             dout("o_kvs2048", [2, 4, 2048, 1024])]

    kT_d = nc.dram_tensor("kT_d", [2, 2, 3, 4, 128, 16, 128], BF16).ap()
    v_d = nc.dram_tensor("v_d", [2, 2, 3, 4, 128, 2080], BF16).ap()
    qs_d = nc.dram_tensor("qs_d", [TS, 1536], F32).ap()

    es = ExitStack()
    with es:
        S = Sched(nc, es)
        sbt = lambda name, shape, dt: es.enter_context(nc.sbuf_tensor(name, shape, dt))
        xres = sbt("xres", [128, 16, D], F32)
        xT = sbt("xT", [128, 8, NT], BF16)
        ar = sbt("arena", [128, 24 * 1024], F32)
        xres_s = sbt("xres_s", [TS, D], F32)
        xT_s = sbt("xT_s", [128, 8, TS], BF16)
        carry = sbt("carry", [128, 2, 4, 2], F32)
        cin_s = sbt("cin_s", [128, 4, 4, 10], F32)
        ident = sbt("ident", [128, 128], BF16)
        steps = sbt("steps", [128, 256], F32)
        tril = sbt("tril", [128, 128], F32)
        bd32 = sbt("bd32", [32, 32], F32)
        sbias = sbt("sbias", [128, 24, 8], F32)
        nsteps = sbt("nsteps", [32, 3, 32], F32)
        hmask = sbt("hmask", [8, 520], F32)
        sel = sbt("sel", [8, 32, 32], BF16)
        sel65 = sbt("sel65", [65, 64], F32)
        stat = sbt("stat", [128, 32], F32)
        stat6 = sbt("stat6", [128, 16, 12], F32)
        mvs = sbt("mvs", [128, 16, 2], F32)
        rstds = sbt("rstds", [128, 16], F32)
        nbias = sbt("nbias", [128, 16], F32)
        cw = sbt("cw", [128, 4, 3], F32)
        ps = es.enter_context(nc.psum_tensor("ps", [128, 4096], F32))

        def A(k, n=1):
            return ar[:, k * 1024:(k + n) * 1024]

        def A16(k, n=1):
            return ar[:, k * 1024:(k + n) * 1024].bitcast(BF16)

        def R(k, n=1):
            return ['a%d' % i for i in range(k, k + n)]

        def bank(b):
            return ps[:, b * 512:(b + 1) * 512]

        def bank16(b):
            return ps[:, b * 512:(b + 1) * 512].bitcast(BF16)

        bank_rr = {'A': [0, 1], 'B': [2, 3], 'C': [4, 5], 'D': [6, 7]}
        bank_i = {'A': 0, 'B': 0, 'C': 0, 'D': 0}

        def nb(kind):
            lst = bank_rr[kind]
            b = lst[bank_i[kind] % len(lst)]
            bank_i[kind] += 1
            return b

        def ld(out, in_, writes, reads=(), q='pool'):
            S.dma(q, lambda e, out=out, in_=in_: e.dma_start(out=out, in_=in_, allow_slow_non_contiguous=True), reads=reads, writes=writes)

        def st(out, in_, reads, writes=()):
            S.dma('sp', lambda e, out=out, in_=in_: e.dma_start(out=out, in_=in_, allow_slow_non_contiguous=True), reads=reads, writes=writes)

        def mm(out, lhsT, rhs, start, stop, reads, writes):
            S.op('pe', lambda e, out=out, lhsT=lhsT, rhs=rhs, start=start, stop=stop:
                 e.matmul(out, lhsT, rhs, start=start, stop=stop), reads=reads, writes=writes)

        def tp(out, in_, idn, reads, writes):
            S.op('pe', lambda e, out=out, in_=in_, idn=idn: e.transpose(out, in_, idn), reads=reads, writes=writes)

        def act(out, in_, func, reads, writes, scale=1.0):
            S.op('act', lambda e, out=out, in_=in_, func=func, scale=scale:
                 e.activation(out=out, in_=in_, func=func, scale=scale), reads=reads, writes=writes)

        def tt(eng, out, in0, in1, op, reads, writes):
            S.op(eng, lambda e, out=out, in0=in0, in1=in1, op=op: e.tensor_tensor(out=out, in0=in0, in1=in1, op=op),
                 reads=reads, writes=writes)

        def ts(eng, out, in0, s1, s2, op0, op1, reads, writes):
            if s2 is None:
                S.op(eng, lambda e, out=out, in0=in0, s1=s1, op0=op0:
                     e.tensor_scalar(out=out, in0=in0, scalar1=s1, scalar2=None, op0=op0), reads=reads, writes=writes)
            else:
                S.op(eng, lambda e, out=out, in0=in0, s1=s1, s2=s2, op0=op0, op1=op1:
                     e.tensor_scalar(out=out, in0=in0, scalar1=s1, scalar2=s2, op0=op0, op1=op1),
                     reads=reads, writes=writes)

        def stt(eng, out, in0, scalar, in1, op0, op1, reads, writes):
            S.op(eng, lambda e, out=out, in0=in0, scalar=scalar, in1=in1, op0=op0, op1=op1:
                 e.scalar_tensor_tensor(out=out, in0=in0, scalar=scalar, in1=in1, op0=op0, op1=op1),
                 reads=reads, writes=writes)

        def cp(eng, out, in_, reads, writes):
            if eng == 'act':
                S.op(eng, lambda e, out=out, in_=in_: e.activation(out=out, in_=in_, func=AF.Copy),
                     reads=reads, writes=writes)
            else:
                S.op(eng, lambda e, out=out, in_=in_: e.tensor_copy(out=out, in_=in_), reads=reads, writes=writes)

        ld(ident[:], c_ident, ['ident'])
        ld(steps[:], c_steps, ['steps'], q='sp')
        ld(tril[:], c_tril, ['tril'], q='sp')
        ld(bd32[:], c_bd32, ['bd32'], q='sp')
        ld(sbias[:].rearrange("p a b -> p (a b)"), c_sbias, ['sbias'], q='sp')
        ld(nsteps[:].rearrange("p a b -> p (a b)"), c_nsteps, ['nsteps'], q='sp')
        ld(hmask[:], c_hmask, ['hmask'], q='sp')
        ld(sel[:].rearrange("p a b -> p (a b)"), c_sel, ['sel'])
        ld(sel65[:], c_sel65, ['sel65'], q='sp')

        for i in range(2 if copy_cache else 0):
            for g in range(3):
                L = WIN[g]
                for s4 in range(4):
                    st(o_kvs[g][i, s4, 0:L - 8, :], cks[g][i, s4, 8:L, :], reads=[])

        class TSet:
            pass

        def mk_prompt():
            t = TSet()
            t.P = 128
            t.ntile = 16
            t.ntok = NT
            t.xr = lambda i: xres[:, i, :]
            t.xr_res = lambda i: 'xr%d' % i
            t.xT = xT
            t.xT_res = lambda tg: 'xT%d' % tg
            t.groups = [(tg * 512, 512) for tg in range(4)]
            t.sample = False
            return t

        def mk_sample():
            t = TSet()
            t.P = TS
            t.ntile = 1
            t.ntok = TS
            t.xr = lambda i: xres_s[:, :]
            t.xr_res = lambda i: 'xrs'
            t.xT = xT_s
            t.xT_res = lambda tg: 'xTs'
            t.groups = [(0, TS)]
            t.sample = True
            return t

        PR = mk_prompt()
        SM = mk_sample()

        def tile_group(T, i):
            return (i * T.P) // 512

        def layernorm_multi(items, P, W, gt, bt, tabres):
            nch = W // 512
            n = len(items)
            for j, (xap, res) in enumerate(items):
                sv = stat6[0:P, j, 0:6 * nch].rearrange("p (c s) -> p c s", c=nch)
                for c in range(nch):
                    S.op('dve', lambda e, o=sv[:, c, :], i=xap[:, c * 512:(c + 1) * 512]: e.bn_stats(out=o, in_=i),
                         reads=[res], writes=['stat'])
                S.op('dve', lambda e, o=mvs[0:P, j, :], i=sv: e.bn_aggr(out=o, in_=i), reads=['stat'], writes=['stat'])
            ts('dve', rstds[0:P, 0:n], mvs[0:P, 0:n, 1], EPS, None, ALU.add, None, ['stat'], ['stat'])
            act(rstds[0:P, 0:n], rstds[0:P, 0:n], AF.Sqrt, ['stat'], ['stat'])
            S.op('dve', lambda e, o=rstds[0:P, 0:n]: e.reciprocal(out=o, in_=o), reads=['stat'], writes=['stat'])
            stt('dve', nbias[0:P, 0:n], mvs[0:P, 0:n, 0], -1.0, rstds[0:P, 0:n], ALU.mult, ALU.mult, ['stat'], ['stat'])
            for j, (xap, res) in enumerate(items):
                S.op('act', lambda e, o=xap, b_=nbias[0:P, j:j + 1], sc=rstds[0:P, j:j + 1]:
                     e.activation(out=o, in_=o, func=AF.Identity, bias=b_, scale=sc), reads=[res, 'stat'], writes=[res])
                tt('dve', xap, xap, gt, ALU.mult, [res] + tabres, [res])
                tt('dve', xap, xap, bt, ALU.add, [res] + tabres, [res])

        def layernorm(xap, P, W, gt, bt, res, tabres):
            layernorm_multi([(xap, res)], P, W, gt, bt, tabres)

        def ln_tables(gsrc, bsrc, W, slot):
            if W == 1024:
                gt, bt = A(slot), A(slot + 1)
                ld(gt, gsrc.partition_broadcast(128), R(slot), q='sp')
                ld(bt, bsrc.partition_broadcast(128), R(slot + 1), q='sp')
                return gt, bt, R(slot, 2)
            gt, bt = A(slot)[:, 0:512], A(slot)[:, 512:1024]
            ld(gt, gsrc.partition_broadcast(128), R(slot), q='sp')
            ld(bt, bsrc.partition_broadcast(128), R(slot), q='sp')
            return gt, bt, R(slot)

        def to_xT(T, i, xbslot):
            P = T.P
            xb = A16(xbslot)[0:P, 0:1024]
            act(xb, T.xr(i), AF.Copy, [T.xr_res(i)], R(xbslot))
            b = nb('D')
            pb = bank16(b)
            for kc in range(8):
                tp(pb[:, kc * P:(kc + 1) * P], xb[:, kc * 128:(kc + 1) * 128], ident[0:P, 0:P],
                   R(xbslot) + ['ident'], ['ps%d' % b])
            c0 = i * P
            cp('dve', T.xT[:, :, c0:c0 + P], pb[:, 0:8 * P].rearrange("p (k t) -> p k t", k=8),
               ['ps%d' % b], [T.xT_res(tile_group(T, i))])

        def mlp(Ts, l):
            gt, bt, tabres = ln_tables(ln2g[l:l + 1, :], ln2b[l:l + 1, :], 1024, 18)
            bank_rr['B'] = [2, 3, 4, 5]
            NFC = 4
            for fb in range(8):
                par = fb % 2
                wup = A16(0 + 2 * par, 2).rearrange("p (k f) -> p k f", k=8)
                wdn = A16(4 + 2 * par, 2).rearrange("p (c d) -> p c d", c=NFC)
                wur, wdr = R(0 + 2 * par, 2), R(4 + 2 * par, 2)
                ld(wup, w_up[l, :, fb * 512:(fb + 1) * 512].rearrange("(k p) f -> p k f", p=128), wur)
                ld(wdn, w_dn[l, fb * 512:(fb + 1) * 512, :].rearrange("(c p) d -> p c d", p=128), wdr)
                for T in Ts:
                    P = T.P
                    if T.sample:
                        hT = A16(23)[:, 0:128].rearrange("p (c t) -> p c t", c=NFC)
                        hres = R(23)
                    else:
                        hT = A16(8 + 4 * par, 4).rearrange("p (c t) -> p c t", c=NFC)
                        hres = R(8 + 4 * par, 4)
                    for gi, (c0, w) in enumerate(T.groups):
                        for fc in range(NFC):
                            b = nb('A')
                            for kc in range(8):
                                mm(bank(b)[:, 0:w], wup[:, kc, fc * 128:(fc + 1) * 128], T.xT[:, kc, c0:c0 + w],
                                   kc == 0, kc == 7, wur + [T.xT_res(gi)], ['ps%d' % b])
                            tslot = 16 if (bank_i['A'] % 2) else 17
                            tmp = A(tslot)[:, 0:512]
                            act(tmp[:, 0:w], bank(b)[:, 0:w], AF.Relu, ['ps%d' % b], R(tslot))
                            act(hT[:, fc, c0:c0 + w], tmp[:, 0:w], AF.Square, R(tslot), hres)
                    for i in range(T.ntile):
                        c0 = i * P
                        for half in range(2):
                            b = nb('B')
                            for fc in range(NFC):
                                mm(bank(b)[0:P, :], hT[:, fc, c0:c0 + P], wdn[:, fc, half * 512:(half + 1) * 512],
                                   fc == 0, fc == NFC - 1, hres + wdr, ['ps%d' % b])
                            xa = T.xr(i)[:, half * 512:(half + 1) * 512]
                            if fb == 0:
                                stt('dve', xa, xa, ALPHA, bank(b)[0:P, :], ALU.mult, ALU.add,
                                    [T.xr_res(i), 'ps%d' % b], [T.xr_res(i)])
                            else:
                                tt('dve', xa, xa, bank(b)[0:P, :], ALU.add, [T.xr_res(i), 'ps%d' % b], [T.xr_res(i)])
            bank_rr['B'] = [2, 3]
            for T in Ts:
                P = T.P
                layernorm_multi([(T.xr(i), T.xr_res(i)) for i in range(T.ntile)], P, 1024, gt[0:P, :], bt[0:P, :],
                                tabres)
                for i in range(T.ntile):
                    to_xT(T, i, 20)

        def proj_res_ln1(T, l, lhs_list, rhs_fn, rd, gt, bt, tabres, xbslot=6):
            P = T.P
            for i in range(T.ntile):
                lh = lhs_list(i)
                for half in range(2):
                    b = nb('B')
                    for j, l_ap in enumerate(lh):
                        mm(bank(b)[0:P, :], l_ap, rhs_fn(j, half), j == 0, j == len(lh) - 1, rd, ['ps%d' % b])
                    xa = T.xr(i)[:, half * 512:(half + 1) * 512]
                    stt('dve', xa, xa, ALPHA, bank(b)[0:P, :], ALU.mult, ALU.add,
                        [T.xr_res(i), 'ps%d' % b], [T.xr_res(i)])
            layernorm_multi([(T.xr(i), T.xr_res(i)) for i in range(T.ntile)], P, 1024, gt[0:P, :], bt[0:P, :], tabres)
            for i in range(T.ntile):
                to_xT(T, i, xbslot)

        def ab_layer(T, l, sg):
            li = l // 2
            P = T.P
            win = A16(13, 10).rearrange("p (k e) -> p k e", k=8)
            for kc in range(8):
                ld(win[:, kc, :], w_in[li, kc * 128:(kc + 1) * 128, :], R(13, 10))
            wout = A16(4, 4).rearrange("p (k e) -> p k e", k=8)
            wsT = A16(3)[:, 0:512].rearrange("p (g i) -> p g i", g=4)
            wtmp = A(3)[:, 256:768].rearrange("p (g i) -> p g i", g=4)
            wraw = A(3)[:, 768:1024].bitcast(BF16).rearrange("p (g i) -> p g i", g=4)
            gv, bv, tabv = ln_tables(lnv_g[li:li + 1, :], lnv_b[li:li + 1, :], 512, 0)
            bsf = A(1)[:, 0:512].rearrange("p (g c) -> p g c", g=4)
            bst = stat[:, 24:28]
            if not T.sample:
                ld(wraw, w_sp[li].rearrange("g i j -> i g j"), R(3))
                b = nb('D')
                for g in range(4):
                    tp(bank16(b)[:, g * 128:(g + 1) * 128], wraw[:, g, :], ident[:], R(3) + ['ident'], ['ps%d' % b])
                tt('dve', wsT, bank16(b)[:, 0:512].rearrange("p (g i) -> p g i", g=4),
                   tril[:].unsqueeze(1).to_broadcast([128, 4, 128]), ALU.mult, ['ps%d' % b, 'tril'], R(3))
                with nc.allow_non_contiguous_dma(reason="tiny transposed bias load"):
                    ld(bst, b_sp[li].rearrange("g i -> i g"), ['bst'], q='sp')
                cp('pool', bsf, bst.unsqueeze(2).to_broadcast([128, 4, 128]), ['bst'], R(1))
            else:
                S.op('pool', lambda e: e.memset(wtmp[0:32, :, 0:32], 0.0), reads=[], writes=R(3))
                with nc.allow_non_contiguous_dma(reason="tiny transposed loads"):
                    for s4 in range(4):
                        for g in range(4):
                            ld(wtmp[s4 * 8:(s4 + 1) * 8, g, s4 * 8:(s4 + 1) * 8],
                               w_sp[li, g, 0:8, 0:8].rearrange("i j -> j i"), R(3), q='sp')
                        ld(bst[s4 * 8:(s4 + 1) * 8, :], b_sp[li, :, 0:8].rearrange("g i -> i g"), ['bst'], q='sp')
                tt('dve', wsT[0:32, :, 0:32], wtmp[0:32, :, 0:32],
                   bd32[:].unsqueeze(1).to_broadcast([32, 4, 32]), ALU.mult, R(3) + ['bd32'], R(3))
                cp('pool', bsf[0:32], bst[0:32].unsqueeze(2).to_broadcast([32, 4, 128]), ['bst'], R(1))
            with nc.allow_non_contiguous_dma(reason="tiny transposed conv weight load"):
                for k in range(3):
                    ld(cw[:, :, k], conv_w[li, k].rearrange("(c p) -> p c", p=128), ['cw'], q='sp')
            gt1, bt1, tab1 = ln_tables(ln1g[l:l + 1, :], ln1b[l:l + 1, :], 1024, 9)

            if T.sample:
                with nc.allow_non_contiguous_dma(reason="tiny transposed conv state load"):
                    for s4 in range(4):
                        for t_ in range(2):
                            ld(cin_s[:, :, s4, t_], sconv[li, s4, t_].rearrange("(c p) -> p c", p=128), ['cin'], q='sp')
            elif sg == 0:
                S.op('pool', lambda e, li=li: e.memset(carry[:, li], 0.0), reads=[], writes=['carry'])

            catT = A16(11, 2).rearrange("p (k t) -> p k t", k=8)
            for gi, (c0, w) in enumerate(T.groups):
                ntl = max(1, w // P)
                def a_stage1(ti):
                    i = gi * 4 + ti if not T.sample else 0
                    tc0 = c0 + ti * P
                    bu, bvk = nb('A'), nb('A')
                    for (bb, col) in ((bu, 0), (bvk, 512)):
                        for kc in range(8):
                            mm(bank(bb)[0:P, :], T.xT[:, kc, tc0:tc0 + P], win[:, kc, col:col + 512],
                               kc == 0, kc == 7, [T.xT_res(gi)] + R(13, 10), ['ps%d' % bb])
                    xs_ = 4 + (ti % 2)
                    xz = A(xs_)[0:P, :]
                    act(xz[:, 0:512], bank(bu)[0:P, :], AF.Gelu_apprx_tanh, ['ps%d' % bu], R(xs_))
                    act(xz[:, 512:1024], bank(bvk)[0:P, :], AF.Gelu_apprx_tanh, ['ps%d' % bvk], R(xs_))
                    vv = xz[:, 512:1024]
                    layernorm(vv, P, 512, gv[0:P, :], bv[0:P, :], 'a%d' % xs_, tabv)
                    if T.sample:
                        st(o_chunkv[li].rearrange("s t c -> (s t) c"), vv, R(xs_))
                    vb = A16(7)[0:P, (ti % 2) * 512:(ti % 2 + 1) * 512]
                    act(vb, vv, AF.Copy, R(xs_), R(7))
                    return dict(ti=ti, xs_=xs_, xz=xz, vb=vb)

                def a_stage2(sa):
                    ti, xs_, xz, vb = sa['ti'], sa['xs_'], sa['xz'], sa['vb']
                    bsp = nb('C')
                    for g in range(4):
                        mm(bank(bsp)[0:P, g * 128:(g + 1) * 128], wsT[0:P, g, 0:P], vb[:, g * 128:(g + 1) * 128],
                           True, True, R(3) + R(7), ['ps%d' % bsp])
                    t3 = A(6)[0:P, 0:512]
                    tt('dve', t3, bank(bsp)[0:P, :], A(1)[0:P, 0:512], ALU.add, ['ps%d' % bsp] + R(1), R(6))
                    ao = A16(6)[0:P, 1024:1536]
                    tt('dve', ao, t3, xz[:, 0:512], ALU.mult, R(6) + R(xs_), R(6))
                    bt_ = nb('D')
                    for ec in range(4):
                        tp(bank16(bt_)[:, ec * P:(ec + 1) * P], ao[:, ec * 128:(ec + 1) * 128], ident[0:P, 0:P],
                           R(6) + ['ident'], ['ps%d' % bt_])
                    cp('act', catT[:, 0:4, ti * P:(ti + 1) * P],
                       bank16(bt_)[:, 0:4 * P].rearrange("p (k t) -> p k t", k=4), ['ps%d' % bt_], R(11, 2))

                pend = None
                for ti in range(ntl):
                    cur = a_stage1(ti)
                    if pend is not None:
                        a_stage2(pend)
                    pend = cur
                a_stage2(pend)
                ld(wout, w_out_ab[li].rearrange("(k p) e -> p k e", p=128), R(4, 4))
                for ec in range(4):
                    bbg, bcg, bh = nb('B'), nb('A'), nb('C')
                    for (bb, col) in ((bbg, 1024), (bcg, 1536), (bh, 2048)):
                        for kc in range(8):
                            mm(bank(bb)[:, 0:w], win[:, kc, col + ec * 128:col + (ec + 1) * 128],
                               T.xT[:, kc, c0:c0 + w], kc == 0, kc == 7, [T.xT_res(gi)] + R(13, 10), ['ps%d' % bb])
                    zh = A(2)[:, 512:1024]
                    act(zh[:, 0:w], bank(bh)[:, 0:w], AF.Copy, ['ps%d' % bh], R(2))
                    co = A(8)[:, 0:512]
                    if not T.sample:
                        ci = A(23)[:, 0:514]
                        cp('act', ci[:, 0:2], carry[:, li, ec, :], ['carry'], ['cin'])
                        tt('dve', ci[:, 2:2 + w], bank(bcg)[:, 0:w], zh[:, 0:w], ALU.mult, ['ps%d' % bcg, 'a2'], ['cin'])
                        ts('dve', co[:, 0:w], ci[:, 2:2 + w], cw[:, ec, 2:3], None, ALU.mult, None, ['cin', 'cw'], R(8))
                        stt('dve', co[:, 0:w], ci[:, 1:1 + w], cw[:, ec, 1:2], co[:, 0:w], ALU.mult, ALU.add,
                            ['cin', 'cw', 'a8'], R(8))
                        stt('dve', co[:, 0:w], ci[:, 0:w], cw[:, ec, 0:1], co[:, 0:w], ALU.mult, ALU.add,
                            ['cin', 'cw', 'a8'], R(8))
                        tt('dve', catT[:, 4 + ec, 0:w], bank(bbg)[:, 0:w], co[:, 0:w], ALU.mult,
                           ['ps%d' % bbg, 'a8'], R(11, 2))
                        if gi == 3 and sg == NSG - 1:
                            with nc.allow_non_contiguous_dma(reason="tiny transposed conv state store"):
                                st(o_convp[li].rearrange("t (c p) -> p c t", p=128)[:, ec, :], ci[:, 512:514], ['cin'])
                        cp('act', carry[:, li, ec, :], ci[:, 512:514], ['cin'], ['carry'])
                    else:
                        ci = cin_s[:, ec, :, :]
                        co3 = co[:, 0:32].rearrange("p (s t) -> p s t", s=4)
                        tt('dve', ci[:, :, 2:10], bank(bcg)[:, 0:32].rearrange("p (s t) -> p s t", s=4),
                           zh[:, 0:32].rearrange("p (s t) -> p s t", s=4), ALU.mult, ['ps%d' % bcg, 'a2'], ['cin'])
                        ts('dve', co3, ci[:, :, 2:10], cw[:, ec, 2:3], None, ALU.mult, None, ['cin', 'cw'], R(8))
                        stt('dve', co3, ci[:, :, 1:9], cw[:, ec, 1:2], co3, ALU.mult, ALU.add,
                            ['cin', 'cw', 'a8'], R(8))
                        stt('dve', co3, ci[:, :, 0:8], cw[:, ec, 0:1], co3, ALU.mult, ALU.add,
                            ['cin', 'cw', 'a8'], R(8))
                        tt('dve', catT[:, 4 + ec, 0:w], bank(bbg)[:, 0:w], co[:, 0:w], ALU.mult,
                           ['ps%d' % bbg, 'a8'], R(11, 2))
                        with nc.allow_non_contiguous_dma(reason="tiny transposed conv state store"):
                            for s4 in range(4):
                                st(o_convs[li, s4].rearrange("t (c p) -> p c t", p=128)[:, ec, :], ci[:, s4, 8:10],
                                   ['cin'])
                for ti in range(ntl):
                    i = gi * 4 + ti if not T.sample else 0
                    for half in range(2):
                        b = nb('B')
                        for k in range(8):
                            mm(bank(b)[0:P, :], catT[:, k, ti * P:(ti + 1) * P], wout[:, k, half * 512:(half + 1) * 512],
                               k == 0, k == 7, R(11, 2) + R(4, 4), ['ps%d' % b])
                        xa = T.xr(i)[:, half * 512:(half + 1) * 512]
                        stt('dve', xa, xa, ALPHA, bank(b)[0:P, :], ALU.mult, ALU.add,
                            [T.xr_res(i), 'ps%d' % b], [T.xr_res(i)])
                gtiles = [(gi * 4 + ti if not T.sample else 0) for ti in range(ntl)]
                layernorm_multi([(T.xr(i), T.xr_res(i)) for i in gtiles], P, 1024, gt1[0:P, :], bt1[0:P, :], tab1)
                for ti in range(ntl):
                    i = gi * 4 + ti if not T.sample else 0
                    to_xT(T, i, 2)

        def blk_cols(g, blk):
            d = DIL[g]
            nper = 16 // d
            r, n = blk // nper, blk % nper
            return r, n, nper

        def c_layer_prompt(l, sg):
            li = l // 2
            T = PR
            par = sg % 2
            if sg == NSG - 1:
                for g in range(3):
                    ntk = WIN[g] // 128
                    wk = A16(0, 2).rearrange("p (k e) -> p k e", k=8)
                    wv = A16(2, 2).rearrange("p (k e) -> p k e", k=8)
                    ld(wk, w_qkv[li, :, 1536 + g * 512:1536 + (g + 1) * 512].rearrange("(k p) e -> p k e", p=128), R(0, 2))
                    ld(wv, w_qkv[li, :, 3072 + g * 512:3072 + (g + 1) * 512].rearrange("(k p) e -> p k e", p=128), R(2, 2))
                    for i in range(16 - ntk, 16):
                        c0 = i * 128
                        ob = A(4 + (i % 2))
                        for (wsrc, wr, off) in ((wk, R(0, 2), 0), (wv, R(2, 2), 512)):
                            b = nb('A')
                            for kc in range(8):
                                mm(bank(b), xT[:, kc, c0:c0 + 128], wsrc[:, kc, :], kc == 0, kc == 7,
                                   ['xT%d' % (i // 4)] + wr, ['ps%d' % b])
                            act(ob[:, off:off + 512], bank(b), AF.Copy, ['ps%d' % b], R(4 + (i % 2)))
                        row0 = (i - (16 - ntk)) * 128
                        st(o_kvp[g][li, row0:row0 + 128, :], ob, R(4 + (i % 2)))
            oT = A16(16, 8).rearrange("p (h t) -> p h t", h=8)
            for hp in range(4):
                acc = A(12, 4).rearrange("p (h t) -> p h t", h=2)
                for g in range(3):
                    d = DIL[g]
                    nper = 16 // d
                    M = NT // d
                    cscale = [-SLOPES[2 * hp + hh] * d for hh in range(2)]
                    wq = A16(0)[:, 0:1024].rearrange("p (k e) -> p k e", k=8)
                    wk = A16(0)[:, 1024:2048].rearrange("p (k e) -> p k e", k=8)
                    wv = A16(1)[:, 0:1024].rearrange("p (k e) -> p k e", k=8)
                    col = g * 512 + hp * 128
                    ld(wq, w_qkv[li, :, col:col + 128].rearrange("(k p) e -> p k e", p=128), R(0))
                    ld(wk, w_qkv[li, :, 1536 + col:1536 + col + 128].rearrange("(k p) e -> p k e", p=128), R(0))
                    ld(wv, w_qkv[li, :, 3072 + col:3072 + col + 128].rearrange("(k p) e -> p k e", p=128), R(1))
                    QT = A16(2)
                    KT = A16(3)
                    V = A16(4, 2)[:, 0:16 * 130].rearrange("p (b h e) -> p b h e", b=16, h=2)
                    S.op('pool', lambda e, V=V: e.memset(V[:, :, :, 64:65], 1.0), reads=[], writes=R(4, 2))
                    QT1 = A16(11)
                    S.op('pool', lambda e, QT=QT: e.memset(QT[64:128, :], 0.0), reads=[], writes=R(2))
                    S.op('pool', lambda e, QT1=QT1: e.memset(QT1[0:64, :], 0.0), reads=[], writes=R(11))
                    QTm = [QT, QT1]
                    for which in range(2):
                        wsrc = wq if which == 0 else wk
                        for tg in range(4):
                            b = nb('A')
                            for kc in range(8):
                                mm(bank(b), wsrc[:, kc, :], xT[:, kc, tg * 512:(tg + 1) * 512], kc == 0, kc == 7,
                                   ['a0', 'xT%d' % tg], ['ps%d' % b])
                            if which == 1:
                                cp('dve', KT[:, tg * 512:(tg + 1) * 512], bank(b), ['ps%d' % b], ['a3'])
                            else:
                                for hh in range(2):
                                    pp = slice(hh * 64, (hh + 1) * 64)
                                    act(QTm[hh][pp, tg * 512:(tg + 1) * 512], bank(b)[pp, :], AF.Copy, ['ps%d' % b],
                                        R(2) if hh == 0 else R(11), scale=0.125)
                    for blk in range(0 if 'noV' in DBG else 16):
                        r, n, _ = blk_cols(g, blk)
                        b = nb('C')
                        start_tok = n * 128 * d + r
                        for kc in range(8):
                            lcols = xT[:, kc, start_tok:start_tok + 127 * d + 1:d]
                            mm(bank(b)[:, 0:128], lcols, wv[:, kc, :], kc == 0, kc == 7,
                               R(1) + ['xT%d' % t_ for t_ in range(4)], ['ps%d' % b])
                        cp('dve', V[:, blk, :, 0:64], bank(b)[:, 0:128].rearrange("p (h e) -> p h e", h=2),
                           ['ps%d' % b], R(4, 2))
                    if sg < NSG - 1 and 'noStore' not in DBG:
                        st(kT_d[li, par, g, hp].rearrange("p b k -> p (b k)"), KT, ['a3'], ['kTd%d' % par])
                        st(v_d[li, par, g, hp], A16(4, 2)[:, 0:2080], R(4, 2), ['vd%d' % par])
                    KTh = A16(6)
                    Vh = A16(7, 2)[:, 0:16 * 130].rearrange("p (b h e) -> p b h e", b=16, h=2)
                    if sg > 0:
                        ld(KTh, kT_d[li, 1 - par, g, hp].rearrange("p b k -> p (b k)"), R(6),
                           reads=['kTd%d' % (1 - par)], q='sp')
                        ld(A16(7, 2)[:, 0:2080], v_d[li, 1 - par, g, hp], R(7, 2), reads=['vd%d' % (1 - par)], q='sp')
                    def att_front(blk):
                        r, n, _ = blk_cols(g, blk)
                        tok0 = n * 128 * d + r
                        span = 127 * d + 1
                        st8 = dict(blk=blk, r=r, n=n, tok0=tok0)
                        if n >= 1:
                            kprev = KT[:, tok0 - 128 * d:tok0 - 128 * d + span:d]
                            st8['vprev'] = V[:, blk - 1]
                            st8['pres'] = ['a3'] + R(4, 2)
                            has_prev = True
                        elif sg > 0:
                            pb_ = r * nper + nper - 1
                            ptok0 = (nper - 1) * 128 * d + r
                            kprev = KTh[:, ptok0:ptok0 + span:d]
                            st8['vprev'] = Vh[:, pb_]
                            st8['pres'] = R(6) + R(7, 2)
                            has_prev = True
                        else:
                            has_prev = False
                        st8['has_prev'] = has_prev
                        kown = KT[:, tok0:tok0 + span:d]
                        bl = nb('B')
                        LT = bank(bl).rearrange("p (h c q) -> p h c q", h=2, c=2)
                        for hh in range(2):
                            qb = QTm[hh][:, tok0:tok0 + span:d]
                            qres = R(2) if hh == 0 else R(11)
                            if has_prev:
                                mm(LT[:, hh, 0, :], kprev, qb, True, True, st8['pres'] + qres, ['ps%d' % bl])
                            mm(LT[:, hh, 1, :], kown, qb, True, True, ['a3'] + qres, ['ps%d' % bl])
                        c_lo = 0 if has_prev else 1
                        tl = A(9 + (blk % 2))[:, 0:512].rearrange("p (h c q) -> p h c q", h=2, c=2)
                        tlr = R(9 + (blk % 2))
                        PTt = A16(9 + (blk % 2))[:, 1024:1536].rearrange("p (h c q) -> p h c q", h=2, c=2)
                        stv = steps[:].rearrange("p (c q) -> p c q", c=2)
                        for hh in range(2):
                            stt('dve', tl[:, hh, c_lo:2, :], stv[:, c_lo:2, :], cscale[hh], LT[:, hh, c_lo:2, :],
                                ALU.mult, ALU.add, ['steps', 'ps%d' % bl], tlr)
                        act(PTt[:, :, c_lo:2, :], tl[:, :, c_lo:2, :], AF.Exp, tlr, tlr)
                        st8['PTt'] = PTt
                        st8['ptr'] = tlr
                        return st8

                    def att_back(st8):
                        blk, r, n = st8['blk'], st8['r'], st8['n']
                        PTt, ptr, has_prev = st8['PTt'], st8['ptr'], st8['has_prev']
                        bo = nb('D')
                        OP = bank(bo)[0:65, 0:256].rearrange("p (h q) -> p h q", h=2)
                        for hh in range(2):
                            if has_prev:
                                mm(OP[:, hh, :], st8['vprev'][:, hh, :], PTt[:, hh, 0, :], True, False,
                                   st8['pres'] + ptr, ['ps%d' % bo])
                            mm(OP[:, hh, :], V[:, blk, hh, :], PTt[:, hh, 1, :], not has_prev, True,
                               R(4, 2) + ptr, ['ps%d' % bo])
                        st_tok = n * 128 * d + r
                        av = acc[0:65, :, st_tok:st_tok + 127 * d + 1:d]
                        if g == 0:
                            cp('dve', av, OP, ['ps%d' % bo], R(12, 4))
                        else:
                            tt('dve', av, OP, av, ALU.add, ['ps%d' % bo] + R(12, 4), R(12, 4))

                    pend = None
                    for blk in range(0 if 'noAttn' in DBG else 16):
                        cur = att_front(blk)
                        if pend is not None:
                            att_back(pend)
                        pend = cur
                    if pend is not None:
                        att_back(pend)
                for hh in range(0 if 'noNorm' in DBG else 2):
                    for tg in range(4):
                        bq = nb('A')
                        mm(bank(bq)[0:64, :], sel65[:], acc[0:65, hh, tg * 512:(tg + 1) * 512], True, True,
                           ['sel65'] + R(12, 4), ['ps%d' % bq])
                        rc = A(1)[0:64, 512:1024]
                        S.op('dve', lambda e, o=rc, i=bank(bq)[0:64, :]: e.reciprocal(out=o, in_=i),
                             reads=['ps%d' % bq], writes=R(1))
                        tt('dve', oT[0:64, 2 * hp + hh, tg * 512:(tg + 1) * 512], acc[0:64, hh, tg * 512:(tg + 1) * 512],
                           rc, ALU.mult, R(12, 4) + R(1), R(16, 8))
            woc = A16(0, 4).rearrange("p (h e) -> p h e", h=8)
            ld(woc[0:64], w_out_c[li].rearrange("(h p) e -> p h e", p=64), R(0, 4))
            gt1, bt1, tab1 = ln_tables(ln1g[l:l + 1, :], ln1b[l:l + 1, :], 1024, 4)
            proj_res_ln1(T, l, lambda i: [oT[0:64, h, i * 128:(i + 1) * 128] for h in range(8)],
                         lambda j, half: woc[0:64, j, half * 512:(half + 1) * 512],
                         R(16, 8) + R(0, 4), gt1, bt1, tab1)

        def c_layer_sample(l):
            li = l // 2
            T = SM
            qtm = A(12, 2)[0:32, 0:1536]
            ktm = A(14, 2)[0:32, 0:1536]
            vtm = A(16, 2)[0:32, 0:1536]
            for which, (dst, dres, scl) in enumerate(((qtm, R(12, 2), 0.125), (ktm, R(14, 2), 1.0), (vtm, R(16, 2), 1.0))):
                for g in range(3):
                    wsl = A16(0, 2).rearrange("p (k e) -> p k e", k=8)
                    col = which * 1536 + g * 512
                    ld(wsl, w_qkv[li, :, col:col + 512].rearrange("(k p) e -> p k e", p=128), R(0, 2))
                    b = nb('A')
                    for kc in range(8):
                        mm(bank(b)[0:32, :], xT_s[:, kc, :], wsl[:, kc, :], kc == 0, kc == 7, ['xTs'] + R(0, 2),
                           ['ps%d' % b])
                    act(dst[:, g * 512:(g + 1) * 512], bank(b)[0:32, :], AF.Copy, ['ps%d' % b], dres, scale=scl)
            for g in range(3):
                L = WIN[g]
                for s4 in range(4):
                    st(o_kvs[g][li, s4, L - 8:L, 0:512], ktm[s4 * 8:(s4 + 1) * 8, g * 512:(g + 1) * 512], R(14, 2))
                    st(o_kvs[g][li, s4, L - 8:L, 512:1024], vtm[s4 * 8:(s4 + 1) * 8, g * 512:(g + 1) * 512], R(16, 2))
            st(qs_d, qtm, R(12, 2), ['qsd'])
            OA = [bank(6)[0:32, 0:260], bank(7)[0:32, 0:260]]
            kb16 = A16(18)[0:32, 0:1536]
            qb16 = A16(19)[0:32, 0:1536]
            act(kb16, ktm, AF.Copy, R(14, 2), R(18))
            act(qb16, qtm, AF.Copy, R(12, 2), R(19))
            vaug = A16(20)[0:32, 0:3 * 8 * 65].rearrange("p (g h e) -> p g h e", g=3, h=8)
            S.op('pool', lambda e: e.memset(vaug[:, :, :, 64:65], 1.0), reads=[], writes=R(20))
            cp('dve', vaug[:, :, :, 0:64], vtm.rearrange("p (g h e) -> p g h e", g=3, h=8), R(16, 2), R(20))
            for g in range(3):
                d = DIL[g]
                for h in range(8):
                    colq = g * 512 + h * 64
                    bt_ = nb('A')
                    tp(bank16(bt_)[0:64, 0:32], kb16[:, colq:colq + 64], ident[0:32, 0:32], R(18) + ['ident'], ['ps%d' % bt_])
                    tp(bank16(bt_)[0:64, 32:64], qb16[:, colq:colq + 64], ident[0:32, 0:32], R(19) + ['ident'], ['ps%d' % bt_])
                    kq = A16(21)[0:64, 0:64]
                    cp('dve', kq, bank16(bt_)[0:64, 0:64], ['ps%d' % bt_], R(21))
                    bl = nb('B')
                    mm(bank(bl)[0:32, 0:32], kq[:, 0:32], kq[:, 32:64], True, True, R(21), ['ps%d' % bl])
                    tl = A(22)[0:32, 0:32]
                    stt('dve', tl, nsteps[:, g, :], -SLOPES[h] * d, bank(bl)[0:32, 0:32], ALU.mult, ALU.add,
                        ['nsteps', 'ps%d' % bl], R(22))
                    pts = A16(23)[0:32, (g * 8 + h) * 32:(g * 8 + h + 1) * 32]
                    act(pts, tl, AF.Exp, R(22), R(23))
            zl = A16(21)[0:8, 1024:1056]
            zr = A16(21)[0:8, 1100:1360]
            S.op('pool', lambda e: e.memset(A16(21)[0:8, 1024:1360], 0.0), reads=[], writes=['a21z'])
            for hb in range(2):
                mm(OA[hb], zl, zr, True, False, ['a21z'], ['ps%d' % (6 + hb)])
            for g in range(3):
                for h in range(8):
                    hb = h // 4
                    pts = A16(23)[0:32, (g * 8 + h) * 32:(g * 8 + h + 1) * 32]
                    mm(OA[hb][:, (h % 4) * 65:(h % 4 + 1) * 65], pts, vaug[:, g, h, :], False, False,
                       R(23) + R(20), ['ps%d' % (6 + hb)])
            S.op('pool', lambda e: e.memset(A(6, 2), 0.0), reads=[], writes=R(6, 2))
            S.op('pool', lambda e: e.memset(A(0, 2), 0.0), reads=[], writes=R(0, 2))
            kvring = [6, 7, 0, 1]
            LTall = A(2, 1)[:, 0:768].rearrange("p (a h) -> p a h", h=8)
            spend = [None]

            def s_back(sb_):
                pt, vres, idx, tok = sb_['pt'], sb_['vres'], sb_['idx'], sb_['tok']
                for hb in range(2):
                    bpv = nb('A')
                    mm(bank(bpv)[0:8, 0:260], pt, A16(9 + (idx % 2))[:, hb * 260:(hb + 1) * 260], True, True,
                       sb_['ptres'] + vres, ['ps%d' % bpv])
                    mk = A16(11)[0:8, hb * 260:(hb + 1) * 260]
                    tt('dve', mk, bank(bpv)[0:8, 0:260], hmask[:, hb * 260:(hb + 1) * 260], ALU.mult,
                       ['ps%d' % bpv, 'hmask'], R(11))
                    mm(OA[hb], sel[:, tok, :], mk, False, False, ['sel'] + R(11), ['ps%d' % (6 + hb)])

            for s4 in range(4):
                for t in range(8):
                    tok = s4 * 8 + t
                    qbc = A(4, 2)[:, 0:1536]
                    ld(qbc, qs_d[tok:tok + 1, :].partition_broadcast(128), R(4, 2), reads=['qsd'], q='sp')
                    for g in range(3):
                        d = DIL[g]
                        L = WIN[g]
                        base = L + t - 128 * d
                        nvalid = 128 - t // d
                        idx = (s4 * 8 + t) * 3 + g
                        kv = A(kvring[idx % 4], 1)
                        kvr = R(kvring[idx % 4])
                        ld(kv[0:nvalid, :], cks[g][li, s4, base:base + (nvalid - 1) * d + 1:d, :], kvr, q='sp')
                        pr = A(8)[:, 0:512]
                        tt('dve', pr, kv[:, 0:512], qbc[:, g * 512:(g + 1) * 512], ALU.mult, kvr + R(4, 2), R(8))
                        S.op('dve', lambda e, o=LTall[:, idx, :], i=pr.rearrange("p (h e) -> p h e", h=8):
                             e.tensor_reduce(out=o, in_=i, axis=mybir.AxisListType.X, op=ALU.add),
                             reads=R(8), writes=R(2))
                        tsl = 3 if idx % 2 == 0 else 19
                        tl = A(tsl)[:, 0:8]
                        tt('dve', tl, LTall[:, idx, :], sbias[:, t * 3 + g, :], ALU.add, R(2) + ['sbias'], R(tsl))
                        pt = A16(tsl)[:, 1024:1032]
                        act(pt, tl, AF.Exp, R(tsl), R(tsl))
                        vb = A16(9 + (idx % 2))[:, 0:520].rearrange("p (h e) -> p h e", h=8)
                        vres = R(9 + (idx % 2))
                        S.op('pool', lambda e, vb=vb: e.memset(vb[:, :, 64:65], 1.0), reads=[], writes=vres)
                        cp('pool', vb[:, :, 0:64], kv[:, 512:1024].rearrange("p (h e) -> p h e", h=8), kvr, vres)
                        cur = dict(pt=pt, ptres=R(tsl), vres=vres, idx=idx, tok=tok)
                        if spend[0] is not None:
                            s_back(spend[0])
                        spend[0] = cur
            s_back(spend[0])
            of = A(12)[0:32, 0:520]
            for hb in range(2):
                mm(OA[hb], zl, zr, False, True, ['a21z'], ['ps%d' % (6 + hb)])
                cp('dve', of[:, hb * 260:(hb + 1) * 260], OA[hb], ['ps%d' % (6 + hb)], R(12))
            ofv = of.rearrange("p (h e) -> p h e", h=8)
            rc = A(13)[0:32, 0:8]
            S.op('dve', lambda e: e.reciprocal(out=rc, in_=ofv[:, :, 64]), reads=R(12), writes=R(13))
            ob = A16(14)[0:32, 0:512]
            tt('dve', ob.rearrange("p (h e) -> p h e", h=8), ofv[:, :, 0:64],
               rc.unsqueeze(2).to_broadcast([32, 8, 64]), ALU.mult, R(12) + R(13), R(14))
            oTs = A16(15)[0:64, 0:256].rearrange("p (h t) -> p h t", h=8)
            bt_ = nb('A')
            for h in range(8):
                tp(bank16(bt_)[0:64, h * 32:(h + 1) * 32], ob[:, h * 64:(h + 1) * 64], ident[0:32, 0:32],
                   R(14) + ['ident'], ['ps%d' % bt_])
            cp('dve', oTs, bank16(bt_)[0:64, 0:256].rearrange("p (h t) -> p h t", h=8), ['ps%d' % bt_], R(15))
            woc = A16(0, 4).rearrange("p (h e) -> p h e", h=8)
            ld(woc[0:64], w_out_c[li].rearrange("(h p) e -> p h e", p=64), R(0, 4))
            gt1, bt1, tab1 = ln_tables(ln1g[l:l + 1, :], ln1b[l:l + 1, :], 1024, 4)
            proj_res_ln1(T, l, lambda i: [oTs[:, h, :] for h in range(8)],
                         lambda j, half: woc[0:64, j, half * 512:(half + 1) * 512],
                         R(15) + R(0, 4), gt1, bt1, tab1)

        def load_x(T, src):
            for i in range(T.ntile):
                ld(T.xr(i), src[i * T.P:(i + 1) * T.P, :], [T.xr_res(i)], q='sp')
                to_xT(T, i, 11)

        def store_y(T, dst):
            for i in range(T.ntile):
                st(dst[i * T.P:(i + 1) * T.P, :], T.xr(i), [T.xr_res(i)])

        for sg in range(nsg if do_prompt else 0):
            with_s = do_sample and sg == 0
            load_x(PR, xp[sg * NT:(sg + 1) * NT, :])
            if with_s:
                load_x(SM, xs)
            for l in range(nlayers):
                if l % 2 == 0:
                    if with_s:
                        ab_layer(SM, l, 0)
                    ab_layer(PR, l, sg)
                else:
                    if with_s:
                        c_layer_sample(l)
                    c_layer_prompt(l, sg)
                mlp([PR, SM] if with_s else [PR], l)
            store_y(PR, yp[sg * NT:(sg + 1) * NT, :])
            if with_s:
                store_y(SM, ys)
        if do_sample and not do_prompt:
            load_x(SM, xs)
            for l in range(nlayers):
                if l % 2 == 0:
                    ab_layer(SM, l, 0)
                else:
                    c_layer_sample(l)
                mlp([SM], l)
            store_y(SM, ys)
        S.finish()
        S.emit()
    return nc


def _consts():
    c = {}
    c["c_ident"] = np.eye(128, dtype=np.float32)
    p = np.arange(128)[:, None]
    q = np.arange(128)[None, :]
    sp = np.where(q <= p, 128 + q - p, BIG).astype(np.float32)
    so = np.where(q >= p, q - p, BIG).astype(np.float32)
    c["c_steps"] = np.concatenate([sp, so], axis=1).astype(np.float32)
    c["c_tril"] = (p <= q).astype(np.float32)
    k = np.arange(32)[:, None]
    qq = np.arange(32)[None, :]
    same = (k // 8) == (qq // 8)
    c["c_bd32"] = (same & ((k % 8) <= (qq % 8))).astype(np.float32)
    sb = np.zeros((128, 24, 8), np.float32)
    for t in range(8):
        for g in range(3):
            d = DIL[g]
            nvalid = 128 - t // d
            for h in range(8):
                v = -SLOPES[h] * d * (128.0 - np.arange(128))
                v[nvalid:] = -BIG
                sb[:, t * 3 + g, h] = v
    c["c_sbias"] = sb.reshape(128, 192)
    ns = np.full((32, 3, 32), BIG, np.float32)
    for kk in range(32):
        for q_ in range(32):
            if kk // 8 != q_ // 8:
                continue
            dt = (q_ % 8) - (kk % 8)
            for g in range(3):
                if dt >= 0 and dt % DIL[g] == 0:
                    ns[kk, g, q_] = dt // DIL[g]
    c["c_nsteps"] = ns.reshape(32, 96)
    hm = np.zeros((8, 8, 65), np.float32)
    for h in range(8):
        hm[h, h, :] = 1.0
    c["c_hmask"] = hm.reshape(8, 520)
    se = np.zeros((8, 32, 32), np.float32)
    for tok in range(32):
        se[:, tok, tok] = 1.0
    c["c_sel"] = se.reshape(8, 1024)
    s65 = np.zeros((65, 64), np.float32)
    s65[64, :] = 1.0
    c["c_sel65"] = s65
    return c


_NC_CACHE = {}


def kernel(x_prompt, x_sample, state_conv, cache_kv_w128, cache_kv_w512, cache_kv_w2048,
           w_in_ab, ln_v_g, ln_v_b, w_spatial, b_spatial, conv_w, w_out_ab,
           w_qkv_c, w_out_c, ln1_g, ln1_b, ln2_g, ln2_b, w_mlp_up, w_mlp_down):
    f = lambda a: np.ascontiguousarray(np.asarray(a, dtype=np.float32))
    if "nc" not in _NC_CACHE:
        _NC_CACHE["nc"] = build_program()
    nc = _NC_CACHE["nc"]
    consts = _consts()
    shared = {
        "w_in_ab": f(w_in_ab), "ln_v_g": f(ln_v_g), "ln_v_b": f(ln_v_b), "w_spatial": f(w_spatial),
        "b_spatial": f(b_spatial), "conv_w": f(conv_w), "w_out_ab": f(w_out_ab), "w_qkv_c": f(w_qkv_c),
        "w_out_c": f(w_out_c), "ln1_g": f(ln1_g), "ln1_b": f(ln1_b), "ln2_g": f(ln2_g), "ln2_b": f(ln2_b),
        "w_mlp_up": f(w_mlp_up), "w_mlp_down": f(w_mlp_down),
    }
    shared.update(consts)
    xpn = f(x_prompt)
    xsn = f(x_sample).reshape(256, D)
    sc = f(state_conv)
    c128 = f(cache_kv_w128).reshape(2, 32, 128, 1024)
    c512 = f(cache_kv_w512).reshape(2, 32, 512, 1024)
    c2048 = f(cache_kv_w2048).reshape(2, 32, 2048, 1024)
    in_maps = []
    for c in range(NCORES):
        m = dict(shared)
        m["xp"] = xpn[c % 2]
        m["xs"] = np.ascontiguousarray(xsn[c * 32:(c + 1) * 32])
        m["sconv"] = np.ascontiguousarray(sc[:, c * 4:(c + 1) * 4])
        m["ck128"] = np.ascontiguousarray(c128[:, c * 4:(c + 1) * 4])
        m["ck512"] = np.ascontiguousarray(c512[:, c * 4:(c + 1) * 4])
        m["ck2048"] = np.ascontiguousarray(c2048[:, c * 4:(c + 1) * 4])
        in_maps.append(m)
    res = run_bass_kernel_spmd(nc, in_maps, core_ids=list(range(NCORES)))
    r = res.results
    y_prompt = np.stack([r[0]["yp"], r[1]["yp"]]).astype(np.float32)
    y_sample = np.concatenate([r[c]["ys"] for c in range(NCORES)], 0).reshape(32, 8, D).astype(np.float32)
    conv_p = np.stack([r[0]["o_convp"], r[1]["o_convp"]], axis=1).astype(np.float32)
    conv_s = np.concatenate([r[c]["o_convs"] for c in range(NCORES)], axis=1).astype(np.float32)
    chunk_v = np.concatenate([r[c]["o_chunkv"] for c in range(NCORES)], axis=1).astype(np.float32)
    outs = [y_prompt, y_sample, conv_p, conv_s, chunk_v]
    for nm, L in (("o_kvp128", 128), ("o_kvp512", 512), ("o_kvp2048", 2048)):
        a = np.stack([r[0][nm], r[1][nm]], axis=1).astype(np.float32)
        outs.append(a.reshape(2, 2, L, 2, 8, 64))
    for nm, L in (("o_kvs128", 128), ("o_kvs512", 512), ("o_kvs2048", 2048)):
        a = np.concatenate([r[c][nm] for c in range(NCORES)], axis=1).astype(np.float32)
        outs.append(a.reshape(2, 32, L, 2, 8, 64))
    return tuple(outs)
```

```python
import numpy as np
from contextlib import ExitStack
import concourse.bass as bass
import concourse.mybir as mybir
from concourse.bass_utils import run_bass_kernel_spmd

F32 = mybir.dt.float32
BF16 = mybir.dt.bfloat16
ALU = mybir.AluOpType
AF = mybir.ActivationFunctionType

D = 1024
SEQ = 8192
NT = 2048
NSG = SEQ // NT
TS = 32
ALPHA = (2.0 * 4) ** 0.25
EPS = 1e-5
BIG = 1.0e30
WIN = (128, 512, 2048)
DIL = (1, 4, 16)
SLOPES = [2.0 ** (-(h + 1)) for h in range(8)]
NCORES = 8
EPOCH = 30000
DBG = set()


class Sched:
    ENG = ['pe', 'act', 'dve', 'pool', 'sp']

    def __init__(self, nc, es, n_dma_sems=40):
        self.nc = nc
        self.es = es
        self.ops = {e: [] for e in self.ENG}
        self.cnt = {e: 0 for e in self.ENG}
        self.epoch = {e: 0 for e in self.ENG}
        self.known = {e: {} for e in self.ENG}
        self.res_w = {}
        self.res_r = {}
        self.sems = {}
        for e in self.ENG:
            self.sems[('e', e, 0)] = es.enter_context(nc.semaphore('s_%s0' % e))
        self.dsem = [es.enter_context(nc.semaphore('d%d' % i)) for i in range(n_dma_sems)]
        self.dcnt = [0] * n_dma_sems
        half = n_dma_sems // 2
        self.dring = {'pool': list(range(0, half)), 'sp': list(range(half, n_dma_sems))}
        self.dnext = {'pool': 0, 'sp': 0}

    def _semof(self, ev):
        if ev[0] == 'd':
            return self.dsem[ev[1]]
        return self.sems[ev[0:3]]

    def _wait(self, engine, ev):
        key = ev[0:3] if ev[0] == 'e' else ev[0:2]
        val = ev[-1]
        if self.known[engine].get(key, 0) >= val:
            return
        self.known[engine][key] = val
        sem = self._semof(ev)
        self.ops[engine].append(lambda eng, sem=sem, val=val: eng.wait_ge(sem, val))

    def _deps(self, engine, reads, writes):
        deps = []
        for r in reads:
            if r in self.res_w:
                deps.append(self.res_w[r])
        for w in writes:
            if w in self.res_w:
                deps.append(self.res_w[w])
            deps.extend(self.res_r.get(w, []))
        for ev in deps:
            if ev[0] == 'e' and ev[1] == engine and engine == 'pe':
                continue
            self._wait(engine, ev)

    def _record(self, ev, reads, writes):
        for r in reads:
            lst = self.res_r.setdefault(r, [])
            if ev[0] == 'e':
                lst[:] = [x for x in lst if not (x[0] == 'e' and x[1] == ev[1] and x[2] == ev[2])]
            lst.append(ev)
        for w in writes:
            self.res_w[w] = ev
            self.res_r[w] = []

    def op(self, engine, fn, reads=(), writes=()):
        self._deps(engine, reads, writes)
        if self.cnt[engine] >= EPOCH:
            self.epoch[engine] += 1
            self.cnt[engine] = 0
            self.sems[('e', engine, self.epoch[engine])] = self.es.enter_context(
                self.nc.semaphore('s_%s%d' % (engine, self.epoch[engine])))
        self.cnt[engine] += 1
        ev = ('e', engine, self.epoch[engine], self.cnt[engine])
        sem = self.sems[ev[0:3]]
        self.ops[engine].append(lambda eng, fn=fn, sem=sem: fn(eng).then_inc(sem, 1))
        self._record(ev, reads, writes)
        return ev

    def dma(self, queue, fn, reads=(), writes=()):
        self._deps(queue, reads, writes)
        ring = self.dring[queue]
        j = ring[self.dnext[queue] % len(ring)]
        self.dnext[queue] += 1
        if self.dcnt[j] > 0:
            self._wait(queue, ('d', j, 16 * self.dcnt[j]))
        self.dcnt[j] += 1
        ev = ('d', j, 16 * self.dcnt[j])
        sem = self.dsem[j]
        self.ops[queue].append(lambda eng, fn=fn, sem=sem: fn(eng).then_inc(sem, 16))
        self._record(ev, reads, writes)
        return ev

    def finish(self):
        for j, c in enumerate(self.dcnt):
            if c > 0:
                self._wait('sp', ('d', j, 16 * c))
        for e in ['pe', 'act', 'dve', 'pool']:
            if self.cnt[e] > 0:
                self._wait('sp', ('e', e, self.epoch[e], self.cnt[e]))

    def emit(self):
        nc = self.nc
        ops = self.ops
        with nc.Block() as block:
            @block.tensor
            def _(e):
                for f in ops['pe']:
                    f(e)

            @block.scalar
            def _(e):
                for f in ops['act']:
                    f(e)

            @block.vector
            def _(e):
                for f in ops['dve']:
                    f(e)

            @block.gpsimd
            def _(e):
                for f in ops['pool']:
                    f(e)

            @block.sync
            def _(e):
                for f in ops['sp']:
                    f(e)


def build_program(nsg=NSG, do_sample=True, do_prompt=True, nlayers=4, copy_cache=True):
    nc = bass.Bass("TRN2", target_bir_lowering=False)

    def din(name, shape, dt=F32):
        return nc.dram_tensor(name, list(shape), dt, kind="ExternalInput").ap()

    def dout(name, shape):
        return nc.dram_tensor(name, list(shape), F32, kind="ExternalOutput").ap()

    xp = din("xp", [SEQ, D])
    xs = din("xs", [TS, D])
    sconv = din("sconv", [2, 4, 2, 512])
    cks = [din("ck128", [2, 4, 128, 1024]), din("ck512", [2, 4, 512, 1024]), din("ck2048", [2, 4, 2048, 1024])]
    w_in = din("w_in_ab", [2, D, 2560])
    lnv_g = din("ln_v_g", [2, 512])
    lnv_b = din("ln_v_b", [2, 512])
    w_sp = din("w_spatial", [2, 4, 128, 128])
    b_sp = din("b_spatial", [2, 4, 128])
    conv_w = din("conv_w", [2, 3, 512])
    w_out_ab = din("w_out_ab", [2, D, D])
    w_qkv = din("w_qkv_c", [2, D, 4608])
    w_out_c = din("w_out_c", [2, 512, D])
    ln1g = din("ln1_g", [4, D])
    ln1b = din("ln1_b", [4, D])
    ln2g = din("ln2_g", [4, D])
    ln2b = din("ln2_b", [4, D])
    w_up = din("w_mlp_up", [4, D, 4096])
    w_dn = din("w_mlp_down", [4, 4096, D])
    c_ident = din("c_ident", [128, 128])
    c_steps = din("c_steps", [128, 256])
    c_tril = din("c_tril", [128, 128])
    c_bd32 = din("c_bd32", [32, 32])
    c_sbias = din("c_sbias", [128, 24 * 8])
    c_nsteps = din("c_nsteps", [32, 3 * 32])
    c_hmask = din("c_hmask", [8, 520])
    c_sel = din("c_sel", [8, 32 * 32])
    c_sel65 = din("c_sel65", [65, 64])

    yp = dout("yp", [SEQ, D])
    ys = dout("ys", [TS, D])
    o_convp = dout("o_convp", [2, 2, 512])
    o_convs = dout("o_convs", [2, 4, 2, 512])
    o_chunkv = dout("o_chunkv", [2, 4, 8, 512])
    o_kvp = [dout("o_kvp128", [2, 128, 1024]), dout("o_kvp512", [2, 512, 1024]), dout("o_kvp2048", [2, 2048, 1024])]
    o_kvs = [dout("o_kvs128", [2, 4, 128, 1024]), dout("o_kvs512", [2, 4, 512, 1024]),
             dout("o_kvs2048", [2, 4, 2048, 1024])]

    kT_d = nc.dram_tensor("kT_d", [2, 2, 3, 4, 128, 16, 128], BF16).ap()
    v_d = nc.dram_tensor("v_d", [2, 2, 3, 4, 128, 2080], BF16).ap()
    qs_d = nc.dram_tensor("qs_d", [TS, 1536], F32).ap()

    es = ExitStack()
    with es:
        S = Sched(nc, es)
        sbt = lambda name, shape, dt: es.enter_context(nc.sbuf_tensor(name, shape, dt))
        xres = sbt("xres", [128, 16, D], F32)
        xT = sbt("xT", [128, 8, NT], BF16)
        ar = sbt("arena", [128, 24 * 1024], F32)
        xres_s = sbt("xres_s", [TS, D], F32)
        xT_s = sbt("xT_s", [128, 8, TS], BF16)
        carry = sbt("carry", [128, 2, 4, 2], F32)
        cin_s = sbt("cin_s", [128, 4, 4, 10], F32)
        ident = sbt("ident", [128, 128], BF16)
        steps = sbt("steps", [128, 256], F32)
        tril = sbt("tril", [128, 128], F32)
        bd32 = sbt("bd32", [32, 32], F32)
        sbias = sbt("sbias", [128, 24, 8], F32)
        nsteps = sbt("nsteps", [32, 3, 32], F32)
        hmask = sbt("hmask", [8, 520], F32)
        sel = sbt("sel", [8, 32, 32], BF16)
        sel65 = sbt("sel65", [65, 64], F32)
        stat = sbt("stat", [128, 32], F32)
        stat6 = sbt("stat6", [128, 16, 12], F32)
        mvs = sbt("mvs", [128, 16, 2], F32)
        rstds = sbt("rstds", [128, 16], F32)
        nbias = sbt("nbias", [128, 16], F32)
        cw = sbt("cw", [128, 4, 3], F32)
        ps = es.enter_context(nc.psum_tensor("ps", [128, 4096], F32))

        def A(k, n=1):
            return ar[:, k * 1024:(k + n) * 1024]

        def A16(k, n=1):
            return ar[:, k * 1024:(k + n) * 1024].bitcast(BF16)

        def R(k, n=1):
            return ['a%d' % i for i in range(k, k + n)]

        def bank(b):
            return ps[:, b * 512:(b + 1) * 512]

        def bank16(b):
            return ps[:, b * 512:(b + 1) * 512].bitcast(BF16)

        bank_rr = {'A': [0, 1], 'B': [2, 3], 'C': [4, 5], 'D': [6, 7]}
        bank_i = {'A': 0, 'B': 0, 'C': 0, 'D': 0}

        def nb(kind):
            lst = bank_rr[kind]
            b = lst[bank_i[kind] % len(lst)]
            bank_i[kind] += 1
            return b

        def ld(out, in_, writes, reads=(), q='pool'):
            S.dma(q, lambda e, out=out, in_=in_: e.dma_start(out=out, in_=in_, allow_slow_non_contiguous=True), reads=reads, writes=writes)

        def st(out, in_, reads, writes=()):
            S.dma('sp', lambda e, out=out, in_=in_: e.dma_start(out=out, in_=in_, allow_slow_non_contiguous=True), reads=reads, writes=writes)

        def mm(out, lhsT, rhs, start, stop, reads, writes):
            S.op('pe', lambda e, out=out, lhsT=lhsT, rhs=rhs, start=start, stop=stop:
                 e.matmul(out, lhsT, rhs, start=start, stop=stop), reads=reads, writes=writes)

        def tp(out, in_, idn, reads, writes):
            S.op('pe', lambda e, out=out, in_=in_, idn=idn: e.transpose(out, in_, idn), reads=reads, writes=writes)

        def act(out, in_, func, reads, writes, scale=1.0):
            S.op('act', lambda e, out=out, in_=in_, func=func, scale=scale:
                 e.activation(out=out, in_=in_, func=func, scale=scale), reads=reads, writes=writes)

        def tt(eng, out, in0, in1, op, reads, writes):
            S.op(eng, lambda e, out=out, in0=in0, in1=in1, op=op: e.tensor_tensor(out=out, in0=in0, in1=in1, op=op),
                 reads=reads, writes=writes)

        def ts(eng, out, in0, s1, s2, op0, op1, reads, writes):
            if s2 is None:
                S.op(eng, lambda e, out=out, in0=in0, s1=s1, op0=op0:
                     e.tensor_scalar(out=out, in0=in0, scalar1=s1, scalar2=None, op0=op0), reads=reads, writes=writes)
            else:
                S.op(eng, lambda e, out=out, in0=in0, s1=s1, s2=s2, op0=op0, op1=op1:
                     e.tensor_scalar(out=out, in0=in0, scalar1=s1, scalar2=s2, op0=op0, op1=op1),
                     reads=reads, writes=writes)

        def stt(eng, out, in0, scalar, in1, op0, op1, reads, writes):
            S.op(eng, lambda e, out=out, in0=in0, scalar=scalar, in1=in1, op0=op0, op1=op1:
                 e.scalar_tensor_tensor(out=out, in0=in0, scalar=scalar, in1=in1, op0=op0, op1=op1),
                 reads=reads, writes=writes)

        def cp(eng, out, in_, reads, writes):
            if eng == 'act':
                S.op(eng, lambda e, out=out, in_=in_: e.activation(out=out, in_=in_, func=AF.Copy),
                     reads=reads, writes=writes)
            else:
                S.op(eng, lambda e, out=out, in_=in_: e.tensor_copy(out=out, in_=in_), reads=reads, writes=writes)

        ld(ident[:], c_ident, ['ident'])
        ld(steps[:], c_steps, ['steps'], q='sp')
        ld(tril[:], c_tril, ['tril'], q='sp')
        ld(bd32[:], c_bd32, ['bd32'], q='sp')
        ld(sbias[:].rearrange("p a b -> p (a b)"), c_sbias, ['sbias'], q='sp')
        ld(nsteps[:].rearrange("p a b -> p (a b)"), c_nsteps, ['nsteps'], q='sp')
        ld(hmask[:], c_hmask, ['hmask'], q='sp')
        ld(sel[:].rearrange("p a b -> p (a b)"), c_sel, ['sel'])
        ld(sel65[:], c_sel65, ['sel65'], q='sp')

        for i in range(2 if copy_cache else 0):
            for g in range(3):
                L = WIN[g]
                for s4 in range(4):
                    st(o_kvs[g][i, s4, 0:L - 8, :], cks[g][i, s4, 8:L, :], reads=[])

        class TSet:
            pass

        def mk_prompt():
            t = TSet()
            t.P = 128
            t.ntile = 16
            t.ntok = NT
            t.xr = lambda i: xres[:, i, :]
            t.xr_res = lambda i: 'xr%d' % i
            t.xT = xT
            t.xT_res = lambda tg: 'xT%d' % tg
            t.groups = [(tg * 512, 512) for tg in range(4)]
            t.sample = False
            return t

        def mk_sample():
            t = TSet()
            t.P = TS
            t.ntile = 1
            t.ntok = TS
            t.xr = lambda i: xres_s[:, :]
            t.xr_res = lambda i: 'xrs'
            t.xT = xT_s
            t.xT_res = lambda tg: 'xTs'
            t.groups = [(0, TS)]
            t.sample = True
            return t

        PR = mk_prompt()
        SM = mk_sample()

        def tile_group(T, i):
            return (i * T.P) // 512

        def layernorm_multi(items, P, W, gt, bt, tabres):
            nch = W // 512
            n = len(items)
            for j, (xap, res) in enumerate(items):
                sv = stat6[0:P, j, 0:6 * nch].rearrange("p (c s) -> p c s", c=nch)
                for c in range(nch):
                    S.op('dve', lambda e, o=sv[:, c, :], i=xap[:, c * 512:(c + 1) * 512]: e.bn_stats(out=o, in_=i),
                         reads=[res], writes=['stat'])
                S.op('dve', lambda e, o=mvs[0:P, j, :], i=sv: e.bn_aggr(out=o, in_=i), reads=['stat'], writes=['stat'])
            ts('dve', rstds[0:P, 0:n], mvs[0:P, 0:n, 1], EPS, None, ALU.add, None, ['stat'], ['stat'])
            act(rstds[0:P, 0:n], rstds[0:P, 0:n], AF.Sqrt, ['stat'], ['stat'])
            S.op('dve', lambda e, o=rstds[0:P, 0:n]: e.reciprocal(out=o, in_=o), reads=['stat'], writes=['stat'])
            stt('dve', nbias[0:P, 0:n], mvs[0:P, 0:n, 0], -1.0, rstds[0:P, 0:n], ALU.mult, ALU.mult, ['stat'], ['stat'])
            for j, (xap, res) in enumerate(items):
                S.op('act', lambda e, o=xap, b_=nbias[0:P, j:j + 1], sc=rstds[0:P, j:j + 1]:
                     e.activation(out=o, in_=o, func=AF.Identity, bias=b_, scale=sc), reads=[res, 'stat'], writes=[res])
                tt('dve', xap, xap, gt, ALU.mult, [res] + tabres, [res])
                tt('dve', xap, xap, bt, ALU.add, [res] + tabres, [res])

        def layernorm(xap, P, W, gt, bt, res, tabres):
            layernorm_multi([(xap, res)], P, W, gt, bt, tabres)

        def ln_tables(gsrc, bsrc, W, slot):
            if W == 1024:
                gt, bt = A(slot), A(slot + 1)
                ld(gt, gsrc.partition_broadcast(128), R(slot), q='sp')
                ld(bt, bsrc.partition_broadcast(128), R(slot + 1), q='sp')
                return gt, bt, R(slot, 2)
            gt, bt = A(slot)[:, 0:512], A(slot)[:, 512:1024]
            ld(gt, gsrc.partition_broadcast(128), R(slot), q='sp')
            ld(bt, bsrc.partition_broadcast(128), R(slot), q='sp')
            return gt, bt, R(slot)

        def to_xT(T, i, xbslot):
            P = T.P
            xb = A16(xbslot)[0:P, 0:1024]
            act(xb, T.xr(i), AF.Copy, [T.xr_res(i)], R(xbslot))
            b = nb('D')
            pb = bank16(b)
            for kc in range(8):
                tp(pb[:, kc * P:(kc + 1) * P], xb[:, kc * 128:(kc + 1) * 128], ident[0:P, 0:P],
                   R(xbslot) + ['ident'], ['ps%d' % b])
            c0 = i * P
            cp('dve', T.xT[:, :, c0:c0 + P], pb[:, 0:8 * P].rearrange("p (k t) -> p k t", k=8),
               ['ps%d' % b], [T.xT_res(tile_group(T, i))])

        def mlp(Ts, l):
            gt, bt, tabres = ln_tables(ln2g[l:l + 1, :], ln2b[l:l + 1, :], 1024, 18)
            bank_rr['B'] = [2, 3, 4, 5]
            NFC = 4
            for fb in range(8):
                par = fb % 2
                wup = A16(0 + 2 * par, 2).rearrange("p (k f) -> p k f", k=8)
                wdn = A16(4 + 2 * par, 2).rearrange("p (c d) -> p c d", c=NFC)
                wur, wdr = R(0 + 2 * par, 2), R(4 + 2 * par, 2)
                ld(wup, w_up[l, :, fb * 512:(fb + 1) * 512].rearrange("(k p) f -> p k f", p=128), wur)
                ld(wdn, w_dn[l, fb * 512:(fb + 1) * 512, :].rearrange("(c p) d -> p c d", p=128), wdr)
                for T in Ts:
                    P = T.P
                    if T.sample:
                        hT = A16(23)[:, 0:128].rearrange("p (c t) -> p c t", c=NFC)
                        hres = R(23)
                    else:
                        hT = A16(8 + 4 * par, 4).rearrange("p (c t) -> p c t", c=NFC)
                        hres = R(8 + 4 * par, 4)
                    for gi, (c0, w) in enumerate(T.groups):
                        for fc in range(NFC):
                            b = nb('A')
                            for kc in range(8):
                                mm(bank(b)[:, 0:w], wup[:, kc, fc * 128:(fc + 1) * 128], T.xT[:, kc, c0:c0 + w],
                                   kc == 0, kc == 7, wur + [T.xT_res(gi)], ['ps%d' % b])
                            tslot = 16 if (bank_i['A'] % 2) else 17
                            tmp = A(tslot)[:, 0:512]
                            act(tmp[:, 0:w], bank(b)[:, 0:w], AF.Relu, ['ps%d' % b], R(tslot))
                            act(hT[:, fc, c0:c0 + w], tmp[:, 0:w], AF.Square, R(tslot), hres)
                    for i in range(T.ntile):
                        c0 = i * P
                        for half in range(2):
                            b = nb('B')
                            for fc in range(NFC):
                                mm(bank(b)[0:P, :], hT[:, fc, c0:c0 + P], wdn[:, fc, half * 512:(half + 1) * 512],
                                   fc == 0, fc == NFC - 1, hres + wdr, ['ps%d' % b])
                            xa = T.xr(i)[:, half * 512:(half + 1) * 512]
                            if fb == 0:
                                stt('dve', xa, xa, ALPHA, bank(b)[0:P, :], ALU.mult, ALU.add,
                                    [T.xr_res(i), 'ps%d' % b], [T.xr_res(i)])
                            else:
                                tt('dve', xa, xa, bank(b)[0:P, :], ALU.add, [T.xr_res(i), 'ps%d' % b], [T.xr_res(i)])
            bank_rr['B'] = [2, 3]
            for T in Ts:
                P = T.P
                layernorm_multi([(T.xr(i), T.xr_res(i)) for i in range(T.ntile)], P, 1024, gt[0:P, :], bt[0:P, :],
                                tabres)
                for i in range(T.ntile):
                    to_xT(T, i, 20)

        def proj_res_ln1(T, l, lhs_list, rhs_fn, rd, gt, bt, tabres, xbslot=6):
            P = T.P
            for i in range(T.ntile):
                lh = lhs_list(i)
                for half in range(2):
                    b = nb('B')
                    for j, l_ap in enumerate(lh):
                        mm(bank(b)[0:P, :], l_ap, rhs_fn(j, half), j == 0, j == len(lh) - 1, rd, ['ps%d' % b])
                    xa = T.xr(i)[:, half * 512:(half + 1) * 512]
                    stt('dve', xa, xa, ALPHA, bank(b)[0:P, :], ALU.mult, ALU.add,
                        [T.xr_res(i), 'ps%d' % b], [T.xr_res(i)])
            layernorm_multi([(T.xr(i), T.xr_res(i)) for i in range(T.ntile)], P, 1024, gt[0:P, :], bt[0:P, :], tabres)
            for i in range(T.ntile):
                to_xT(T, i, xbslot)

        def ab_layer(T, l, sg):
            li = l // 2
            P = T.P
            win_a = A16(8, 4).rearrange("p (k e) -> p k e", k=8)
            win_b = A16(17, 6).rearrange("p (k e) -> p k e", k=8)
            ld(win_a, w_in[li, :, 0:1024].rearrange("(k p) e -> p k e", p=128), R(8, 4))
            ld(win_b, w_in[li, :, 1024:2560].rearrange("(k p) e -> p k e", p=128), R(17, 6))
            wout = A16(4, 4).rearrange("p (k e) -> p k e", k=8)
            wsT = A16(3)[:, 0:512].rearrange("p (g i) -> p g i", g=4)
            wtmp = A(3)[:, 256:768].rearrange("p (g i) -> p g i", g=4)
            wraw = A(3)[:, 768:1024].bitcast(BF16).rearrange("p (g i) -> p g i", g=4)
            gv, bv, tabv = ln_tables(lnv_g[li:li + 1, :], lnv_b[li:li + 1, :], 512, 0)
            bsf = A(1)[:, 0:512].rearrange("p (g c) -> p g c", g=4)
            bst = stat[:, 24:28]
            if not T.sample:
                ld(wraw, w_sp[li].rearrange("g i j -> i g j"), R(3))
                b = nb('D')
                for g in range(4):
                    tp(bank16(b)[:, g * 128:(g + 1) * 128], wraw[:, g, :], ident[:], R(3) + ['ident'], ['ps%d' % b])
                tt('dve', wsT, bank16(b)[:, 0:512].rearrange("p (g i) -> p g i", g=4),
                   tril[:].unsqueeze(1).to_broadcast([128, 4, 128]), ALU.mult, ['ps%d' % b, 'tril'], R(3))
                with nc.allow_non_contiguous_dma(reason="tiny transposed bias load"):
                    ld(bst, b_sp[li].rearrange("g i -> i g"), ['bst'], q='sp')
                cp('pool', bsf, bst.unsqueeze(2).to_broadcast([128, 4, 128]), ['bst'], R(1))
            else:
                S.op('pool', lambda e: e.memset(wtmp[0:32, :, 0:32], 0.0), reads=[], writes=R(3))
                with nc.allow_non_contiguous_dma(reason="tiny transposed loads"):
                    for s4 in range(4):
                        for g in range(4):
                            ld(wtmp[s4 * 8:(s4 + 1) * 8, g, s4 * 8:(s4 + 1) * 8],
                               w_sp[li, g, 0:8, 0:8].rearrange("i j -> j i"), R(3), q='sp')
                        ld(bst[s4 * 8:(s4 + 1) * 8, :], b_sp[li, :, 0:8].rearrange("g i -> i g"), ['bst'], q='sp')
                tt('dve', wsT[0:32, :, 0:32], wtmp[0:32, :, 0:32],
                   bd32[:].unsqueeze(1).to_broadcast([32, 4, 32]), ALU.mult, R(3) + ['bd32'], R(3))
                cp('pool', bsf[0:32], bst[0:32].unsqueeze(2).to_broadcast([32, 4, 128]), ['bst'], R(1))
            with nc.allow_non_contiguous_dma(reason="tiny transposed conv weight load"):
                for k in range(3):
                    ld(cw[:, :, k], conv_w[li, k].rearrange("(c p) -> p c", p=128), ['cw'], q='sp')
            gt1, bt1, tab1 = ln_tables(ln1g[l:l + 1, :], ln1b[l:l + 1, :], 1024, 13)

            if T.sample:
                with nc.allow_non_contiguous_dma(reason="tiny transposed conv state load"):
                    for s4 in range(4):
                        for t_ in range(2):
                            ld(cin_s[:, :, s4, t_], sconv[li, s4, t_].rearrange("(c p) -> p c", p=128), ['cin'], q='sp')
            elif sg == 0:
                S.op('pool', lambda e, li=li: e.memset(carry[:, li], 0.0), reads=[], writes=['carry'])

            catT = A16(15, 2).rearrange("p (k t) -> p k t", k=8)
            for gi, (c0, w) in enumerate(T.groups):
                ntl = max(1, w // P)
                def a_stage1(ti):
                    i = gi * 4 + ti if not T.sample else 0
                    tc0 = c0 + ti * P
                    bu, bvk = nb('A'), nb('A')
                    for (bb, col) in ((bu, 0), (bvk, 512)):
                        for kc in range(8):
                            mm(bank(bb)[0:P, :], T.xT[:, kc, tc0:tc0 + P], win_a[:, kc, col:col + 512],
                               kc == 0, kc == 7, [T.xT_res(gi)] + R(8, 4), ['ps%d' % bb])
                    xs_ = 4 + (ti % 2)
                    xz = A(xs_)[0:P, :]
                    act(xz[:, 0:512], bank(bu)[0:P, :], AF.Gelu_apprx_tanh, ['ps%d' % bu], R(xs_))
                    act(xz[:, 512:1024], bank(bvk)[0:P, :], AF.Gelu_apprx_tanh, ['ps%d' % bvk], R(xs_))
                    vv = xz[:, 512:1024]
                    layernorm(vv, P, 512, gv[0:P, :], bv[0:P, :], 'a%d' % xs_, tabv)
                    if T.sample:
                        st(o_chunkv[li].rearrange("s t c -> (s t) c"), vv, R(xs_))
                    vb = A16(7)[0:P, (ti % 2) * 512:(ti % 2 + 1) * 512]
                    act(vb, vv, AF.Copy, R(xs_), R(7))
                    return dict(ti=ti, xs_=xs_, xz=xz, vb=vb)

                def a_stage2(sa):
                    ti, xs_, xz, vb = sa['ti'], sa['xs_'], sa['xz'], sa['vb']
                    bsp = nb('C')
                    for g in range(4):
                        mm(bank(bsp)[0:P, g * 128:(g + 1) * 128], wsT[0:P, g, 0:P], vb[:, g * 128:(g + 1) * 128],
                           True, True, R(3) + R(7), ['ps%d' % bsp])
                    t3 = A(6)[0:P, 0:512]
                    tt('dve', t3, bank(bsp)[0:P, :], A(1)[0:P, 0:512], ALU.add, ['ps%d' % bsp] + R(1), R(6))
                    ao = A16(6)[0:P, 1024:1536]
                    tt('dve', ao, t3, xz[:, 0:512], ALU.mult, R(6) + R(xs_), R(6))
                    bt_ = nb('D')
                    for ec in range(4):
                        tp(bank16(bt_)[:, ec * P:(ec + 1) * P], ao[:, ec * 128:(ec + 1) * 128], ident[0:P, 0:P],
                           R(6) + ['ident'], ['ps%d' % bt_])
                    cp('act', catT[:, 0:4, ti * P:(ti + 1) * P],
                       bank16(bt_)[:, 0:4 * P].rearrange("p (k t) -> p k t", k=4), ['ps%d' % bt_], R(15, 2))

                pend = None
                for ti in range(ntl):
                    cur = a_stage1(ti)
                    if pend is not None:
                        a_stage2(pend)
                    pend = cur
                a_stage2(pend)
                ld(wout, w_out_ab[li].rearrange("(k p) e -> p k e", p=128), R(4, 4))
                for ec in range(4):
                    bbg, bcg, bh = nb('B'), nb('A'), nb('C')
                    for (bb, col) in ((bbg, 1024), (bcg, 1536), (bh, 2048)):
                        for kc in range(8):
                            mm(bank(bb)[:, 0:w], win_b[:, kc, col - 1024 + ec * 128:col - 1024 + (ec + 1) * 128],
                               T.xT[:, kc, c0:c0 + w], kc == 0, kc == 7, [T.xT_res(gi)] + R(17, 6), ['ps%d' % bb])
                    zh = A(2)[:, 512:1024]
                    act(zh[:, 0:w], bank(bh)[:, 0:w], AF.Copy, ['ps%d' % bh], R(2))
                    co = A(12)[:, 0:512]
                    if not T.sample:
                        ci = A(23)[:, 0:514]
                        cp('act', ci[:, 0:2], carry[:, li, ec, :], ['carry'], ['cin'])
                        tt('dve', ci[:, 2:2 + w], bank(bcg)[:, 0:w], zh[:, 0:w], ALU.mult, ['ps%d' % bcg, 'a2'], ['cin'])
                        ts('dve', co[:, 0:w], ci[:, 2:2 + w], cw[:, ec, 2:3], None, ALU.mult, None, ['cin', 'cw'], R(12))
                        stt('dve', co[:, 0:w], ci[:, 1:1 + w], cw[:, ec, 1:2], co[:, 0:w], ALU.mult, ALU.add,
                            ['cin', 'cw', 'a12'], R(12))
                        stt('dve', co[:, 0:w], ci[:, 0:w], cw[:, ec, 0:1], co[:, 0:w], ALU.mult, ALU.add,
                            ['cin', 'cw', 'a12'], R(12))
                        tt('dve', catT[:, 4 + ec, 0:w], bank(bbg)[:, 0:w], co[:, 0:w], ALU.mult,
                           ['ps%d' % bbg, 'a12'], R(15, 2))
                        if gi == 3 and sg == NSG - 1:
                            with nc.allow_non_contiguous_dma(reason="tiny transposed conv state store"):
                                st(o_convp[li].rearrange("t (c p) -> p c t", p=128)[:, ec, :], ci[:, 512:514], ['cin'])
                        cp('act', carry[:, li, ec, :], ci[:, 512:514], ['cin'], ['carry'])
                    else:
                        ci = cin_s[:, ec, :, :]
                        co3 = co[:, 0:32].rearrange("p (s t) -> p s t", s=4)
                        tt('dve', ci[:, :, 2:10], bank(bcg)[:, 0:32].rearrange("p (s t) -> p s t", s=4),
                           zh[:, 0:32].rearrange("p (s t) -> p s t", s=4), ALU.mult, ['ps%d' % bcg, 'a2'], ['cin'])
                        ts('dve', co3, ci[:, :, 2:10], cw[:, ec, 2:3], None, ALU.mult, None, ['cin', 'cw'], R(12))
                        stt('dve', co3, ci[:, :, 1:9], cw[:, ec, 1:2], co3, ALU.mult, ALU.add,
                            ['cin', 'cw', 'a12'], R(12))
                        stt('dve', co3, ci[:, :, 0:8], cw[:, ec, 0:1], co3, ALU.mult, ALU.add,
                            ['cin', 'cw', 'a12'], R(12))
                        tt('dve', catT[:, 4 + ec, 0:w], bank(bbg)[:, 0:w], co[:, 0:w], ALU.mult,
                           ['ps%d' % bbg, 'a12'], R(15, 2))
                        with nc.allow_non_contiguous_dma(reason="tiny transposed conv state store"):
                            for s4 in range(4):
                                st(o_convs[li, s4].rearrange("t (c p) -> p c t", p=128)[:, ec, :], ci[:, s4, 8:10],
                                   ['cin'])
                for ti in range(ntl):
                    i = gi * 4 + ti if not T.sample else 0
                    for half in range(2):
                        b = nb('B')
                        for k in range(8):
                            mm(bank(b)[0:P, :], catT[:, k, ti * P:(ti + 1) * P], wout[:, k, half * 512:(half + 1) * 512],
                               k == 0, k == 7, R(15, 2) + R(4, 4), ['ps%d' % b])
                        xa = T.xr(i)[:, half * 512:(half + 1) * 512]
                        stt('dve', xa, xa, ALPHA, bank(b)[0:P, :], ALU.mult, ALU.add,
                            [T.xr_res(i), 'ps%d' % b], [T.xr_res(i)])
                gtiles = [(gi * 4 + ti if not T.sample else 0) for ti in range(ntl)]
                layernorm_multi([(T.xr(i), T.xr_res(i)) for i in gtiles], P, 1024, gt1[0:P, :], bt1[0:P, :], tab1)
                for ti in range(ntl):
                    i = gi * 4 + ti if not T.sample else 0
                    to_xT(T, i, 2)

        def blk_cols(g, blk):
            d = DIL[g]
            nper = 16 // d
            r, n = blk // nper, blk % nper
            return r, n, nper

        def c_layer_prompt(l, sg):
            li = l // 2
            T = PR
            par = sg % 2
            if sg == NSG - 1:
                for g in range(3):
                    ntk = WIN[g] // 128
                    wk = A16(0, 2).rearrange("p (k e) -> p k e", k=8)
                    wv = A16(2, 2).rearrange("p (k e) -> p k e", k=8)
                    ld(wk, w_qkv[li, :, 1536 + g * 512:1536 + (g + 1) * 512].rearrange("(k p) e -> p k e", p=128), R(0, 2))
                    ld(wv, w_qkv[li, :, 3072 + g * 512:3072 + (g + 1) * 512].rearrange("(k p) e -> p k e", p=128), R(2, 2))
                    for i in range(16 - ntk, 16):
                        c0 = i * 128
                        ob = A(4 + (i % 2))
                        for (wsrc, wr, off) in ((wk, R(0, 2), 0), (wv, R(2, 2), 512)):
                            b = nb('A')
                            for kc in range(8):
                                mm(bank(b), xT[:, kc, c0:c0 + 128], wsrc[:, kc, :], kc == 0, kc == 7,
                                   ['xT%d' % (i // 4)] + wr, ['ps%d' % b])
                            act(ob[:, off:off + 512], bank(b), AF.Copy, ['ps%d' % b], R(4 + (i % 2)))
                        row0 = (i - (16 - ntk)) * 128
                        st(o_kvp[g][li, row0:row0 + 128, :], ob, R(4 + (i % 2)))
            oT = A16(16, 8).rearrange("p (h t) -> p h t", h=8)
            for hp in range(4):
                acc = A(12, 4).rearrange("p (h t) -> p h t", h=2)
                for g in range(3):
                    d = DIL[g]
                    nper = 16 // d
                    M = NT // d
                    cscale = [-SLOPES[2 * hp + hh] * d for hh in range(2)]
                    wq = A16(0)[:, 0:1024].rearrange("p (k e) -> p k e", k=8)
                    wk = A16(0)[:, 1024:2048].rearrange("p (k e) -> p k e", k=8)
                    wv = A16(1)[:, 0:1024].rearrange("p (k e) -> p k e", k=8)
                    col = g * 512 + hp * 128
                    ld(wq, w_qkv[li, :, col:col + 128].rearrange("(k p) e -> p k e", p=128), R(0))
                    ld(wk, w_qkv[li, :, 1536 + col:1536 + col + 128].rearrange("(k p) e -> p k e", p=128), R(0))
                    ld(wv, w_qkv[li, :, 3072 + col:3072 + col + 128].rearrange("(k p) e -> p k e", p=128), R(1))
                    QT = A16(2)
                    KT = A16(3)
                    V = A16(4, 2)[:, 0:16 * 130].rearrange("p (b h e) -> p b h e", b=16, h=2)
                    S.op('pool', lambda e, V=V: e.memset(V[:, :, :, 64:65], 1.0), reads=[], writes=R(4, 2))
                    QT1 = A16(11)
                    S.op('pool', lambda e, QT=QT: e.memset(QT[64:128, :], 0.0), reads=[], writes=R(2))
                    S.op('pool', lambda e, QT1=QT1: e.memset(QT1[0:64, :], 0.0), reads=[], writes=R(11))
                    QTm = [QT, QT1]
                    for which in range(2):
                        wsrc = wq if which == 0 else wk
                        for tg in range(4):
                            b = nb('A')
                            for kc in range(8):
                                mm(bank(b), wsrc[:, kc, :], xT[:, kc, tg * 512:(tg + 1) * 512], kc == 0, kc == 7,
                                   ['a0', 'xT%d' % tg], ['ps%d' % b])
                            if which == 1:
                                cp('dve', KT[:, tg * 512:(tg + 1) * 512], bank(b), ['ps%d' % b], ['a3'])
                            else:
                                for hh in range(2):
                                    pp = slice(hh * 64, (hh + 1) * 64)
                                    act(QTm[hh][pp, tg * 512:(tg + 1) * 512], bank(b)[pp, :], AF.Copy, ['ps%d' % b],
                                        R(2) if hh == 0 else R(11), scale=0.125)
                    for blk in range(0 if 'noV' in DBG else 16):
                        r, n, _ = blk_cols(g, blk)
                        b = nb('C')
                        start_tok = n * 128 * d + r
                        for kc in range(8):
                            lcols = xT[:, kc, start_tok:start_tok + 127 * d + 1:d]
                            mm(bank(b)[:, 0:128], lcols, wv[:, kc, :], kc == 0, kc == 7,
                               R(1) + ['xT%d' % t_ for t_ in range(4)], ['ps%d' % b])
                        cp('dve', V[:, blk, :, 0:64], bank(b)[:, 0:128].rearrange("p (h e) -> p h e", h=2),
                           ['ps%d' % b], R(4, 2))
                    if sg < NSG - 1 and 'noStore' not in DBG:
                        st(kT_d[li, par, g, hp].rearrange("p b k -> p (b k)"), KT, ['a3'], ['kTd%d' % par])
                        st(v_d[li, par, g, hp], A16(4, 2)[:, 0:2080], R(4, 2), ['vd%d' % par])
                    KTh = A16(6)
                    Vh = A16(7, 2)[:, 0:16 * 130].rearrange("p (b h e) -> p b h e", b=16, h=2)
                    if sg > 0:
                        ld(KTh, kT_d[li, 1 - par, g, hp].rearrange("p b k -> p (b k)"), R(6),
                           reads=['kTd%d' % (1 - par)], q='sp')
                        ld(A16(7, 2)[:, 0:2080], v_d[li, 1 - par, g, hp], R(7, 2), reads=['vd%d' % (1 - par)], q='sp')
                    def att_front(blk):
                        r, n, _ = blk_cols(g, blk)
                        tok0 = n * 128 * d + r
                        span = 127 * d + 1
                        st8 = dict(blk=blk, r=r, n=n, tok0=tok0)
                        if n >= 1:
                            kprev = KT[:, tok0 - 128 * d:tok0 - 128 * d + span:d]
                            st8['vprev'] = V[:, blk - 1]
                            st8['pres'] = ['a3'] + R(4, 2)
                            has_prev = True
                        elif sg > 0:
                            pb_ = r * nper + nper - 1
                            ptok0 = (nper - 1) * 128 * d + r
                            kprev = KTh[:, ptok0:ptok0 + span:d]
                            st8['vprev'] = Vh[:, pb_]
                            st8['pres'] = R(6) + R(7, 2)
                            has_prev = True
                        else:
                            has_prev = False
                        st8['has_prev'] = has_prev
                        kown = KT[:, tok0:tok0 + span:d]
                        bl = nb('B')
                        LT = bank(bl).rearrange("p (h c q) -> p h c q", h=2, c=2)
                        for hh in range(2):
                            qb = QTm[hh][:, tok0:tok0 + span:d]
                            qres = R(2) if hh == 0 else R(11)
                            if has_prev:
                                mm(LT[:, hh, 0, :], kprev, qb, True, True, st8['pres'] + qres, ['ps%d' % bl])
                            mm(LT[:, hh, 1, :], kown, qb, True, True, ['a3'] + qres, ['ps%d' % bl])
                        c_lo = 0 if has_prev else 1
                        tl = A(9 + (blk % 2))[:, 0:512].rearrange("p (h c q) -> p h c q", h=2, c=2)
                        tlr = R(9 + (blk % 2))
                        PTt = A16(9 + (blk % 2))[:, 1024:1536].rearrange("p (h c q) -> p h c q", h=2, c=2)
                        stv = steps[:].rearrange("p (c q) -> p c q", c=2)
                        for hh in range(2):
                            stt('dve', tl[:, hh, c_lo:2, :], stv[:, c_lo:2, :], cscale[hh], LT[:, hh, c_lo:2, :],
                                ALU.mult, ALU.add, ['steps', 'ps%d' % bl], tlr)
                        act(PTt[:, :, c_lo:2, :], tl[:, :, c_lo:2, :], AF.Exp, tlr, tlr)
                        st8['PTt'] = PTt
                        st8['ptr'] = tlr
                        return st8

                    def att_back(st8):
                        blk, r, n = st8['blk'], st8['r'], st8['n']
                        PTt, ptr, has_prev = st8['PTt'], st8['ptr'], st8['has_prev']
                        bo = nb('D')
                        OP = bank(bo)[0:65, 0:256].rearrange("p (h q) -> p h q", h=2)
                        for hh in range(2):
                            if has_prev:
                                mm(OP[:, hh, :], st8['vprev'][:, hh, :], PTt[:, hh, 0, :], True, False,
                                   st8['pres'] + ptr, ['ps%d' % bo])
                            mm(OP[:, hh, :], V[:, blk, hh, :], PTt[:, hh, 1, :], not has_prev, True,
                               R(4, 2) + ptr, ['ps%d' % bo])
                        st_tok = n * 128 * d + r
                        av = acc[0:65, :, st_tok:st_tok + 127 * d + 1:d]
                        if g == 0:
                            cp('dve', av, OP, ['ps%d' % bo], R(12, 4))
                        else:
                            tt('dve', av, OP, av, ALU.add, ['ps%d' % bo] + R(12, 4), R(12, 4))

                    pend = None
                    for blk in range(0 if 'noAttn' in DBG else 16):
                        cur = att_front(blk)
                        if pend is not None:
                            att_back(pend)
                        pend = cur
                    if pend is not None:
                        att_back(pend)
                for hh in range(0 if 'noNorm' in DBG else 2):
                    for tg in range(4):
                        bq = nb('A')
                        mm(bank(bq)[0:64, :], sel65[:], acc[0:65, hh, tg * 512:(tg + 1) * 512], True, True,
                           ['sel65'] + R(12, 4), ['ps%d' % bq])
                        rc = A(1)[0:64, 512:1024]
                        S.op('dve', lambda e, o=rc, i=bank(bq)[0:64, :]: e.reciprocal(out=o, in_=i),
                             reads=['ps%d' % bq], writes=R(1))
                        tt('dve', oT[0:64, 2 * hp + hh, tg * 512:(tg + 1) * 512], acc[0:64, hh, tg * 512:(tg + 1) * 512],
                           rc, ALU.mult, R(12, 4) + R(1), R(16, 8))
            woc = A16(0, 4).rearrange("p (h e) -> p h e", h=8)
            ld(woc[0:64], w_out_c[li].rearrange("(h p) e -> p h e", p=64), R(0, 4))
            gt1, bt1, tab1 = ln_tables(ln1g[l:l + 1, :], ln1b[l:l + 1, :], 1024, 4)
            proj_res_ln1(T, l, lambda i: [oT[0:64, h, i * 128:(i + 1) * 128] for h in range(8)],
                         lambda j, half: woc[0:64, j, half * 512:(half + 1) * 512],
                         R(16, 8) + R(0, 4), gt1, bt1, tab1)

        def c_layer_sample(l):
            li = l // 2
            T = SM
            qtm = A(12, 2)[0:32, 0:1536]
            ktm = A(14, 2)[0:32, 0:1536]
            vtm = A(16, 2)[0:32, 0:1536]
            for which, (dst, dres, scl) in enumerate(((qtm, R(12, 2), 0.125), (ktm, R(14, 2), 1.0), (vtm, R(16, 2), 1.0))):
                for g in range(3):
                    wsl = A16(0, 2).rearrange("p (k e) -> p k e", k=8)
                    col = which * 1536 + g * 512
                    ld(wsl, w_qkv[li, :, col:col + 512].rearrange("(k p) e -> p k e", p=128), R(0, 2))
                    b = nb('A')
                    for kc in range(8):
                        mm(bank(b)[0:32, :], xT_s[:, kc, :], wsl[:, kc, :], kc == 0, kc == 7, ['xTs'] + R(0, 2),
                           ['ps%d' % b])
                    act(dst[:, g * 512:(g + 1) * 512], bank(b)[0:32, :], AF.Copy, ['ps%d' % b], dres, scale=scl)
            for g in range(3):
                L = WIN[g]
                for s4 in range(4):
                    st(o_kvs[g][li, s4, L - 8:L, 0:512], ktm[s4 * 8:(s4 + 1) * 8, g * 512:(g + 1) * 512], R(14, 2))
                    st(o_kvs[g][li, s4, L - 8:L, 512:1024], vtm[s4 * 8:(s4 + 1) * 8, g * 512:(g + 1) * 512], R(16, 2))
            st(qs_d, qtm, R(12, 2), ['qsd'])
            OA = [bank(6)[0:32, 0:260], bank(7)[0:32, 0:260]]
            kb16 = A16(18)[0:32, 0:1536]
            qb16 = A16(19)[0:32, 0:1536]
            act(kb16, ktm, AF.Copy, R(14, 2), R(18))
            act(qb16, qtm, AF.Copy, R(12, 2), R(19))
            vaug = A16(20)[0:32, 0:3 * 8 * 65].rearrange("p (g h e) -> p g h e", g=3, h=8)
            S.op('pool', lambda e: e.memset(vaug[:, :, :, 64:65], 1.0), reads=[], writes=R(20))
            cp('dve', vaug[:, :, :, 0:64], vtm.rearrange("p (g h e) -> p g h e", g=3, h=8), R(16, 2), R(20))
            for g in range(3):
                d = DIL[g]
                for h in range(8):
                    colq = g * 512 + h * 64
                    bt_ = nb('A')
                    tp(bank16(bt_)[0:64, 0:32], kb16[:, colq:colq + 64], ident[0:32, 0:32], R(18) + ['ident'], ['ps%d' % bt_])
                    tp(bank16(bt_)[0:64, 32:64], qb16[:, colq:colq + 64], ident[0:32, 0:32], R(19) + ['ident'], ['ps%d' % bt_])
                    kq = A16(21)[0:64, 0:64]
                    cp('dve', kq, bank16(bt_)[0:64, 0:64], ['ps%d' % bt_], R(21))
                    bl = nb('B')
                    mm(bank(bl)[0:32, 0:32], kq[:, 0:32], kq[:, 32:64], True, True, R(21), ['ps%d' % bl])
                    tl = A(22)[0:32, 0:32]
                    stt('dve', tl, nsteps[:, g, :], -SLOPES[h] * d, bank(bl)[0:32, 0:32], ALU.mult, ALU.add,
                        ['nsteps', 'ps%d' % bl], R(22))
                    pts = A16(23)[0:32, (g * 8 + h) * 32:(g * 8 + h + 1) * 32]
                    act(pts, tl, AF.Exp, R(22), R(23))
            zl = A16(21)[0:8, 1024:1056]
            zr = A16(21)[0:8, 1100:1360]
            S.op('pool', lambda e: e.memset(A16(21)[0:8, 1024:1360], 0.0), reads=[], writes=['a21z'])
            for hb in range(2):
                mm(OA[hb], zl, zr, True, False, ['a21z'], ['ps%d' % (6 + hb)])
            for g in range(3):
                for h in range(8):
                    hb = h // 4
                    pts = A16(23)[0:32, (g * 8 + h) * 32:(g * 8 + h + 1) * 32]
                    mm(OA[hb][:, (h % 4) * 65:(h % 4 + 1) * 65], pts, vaug[:, g, h, :], False, False,
                       R(23) + R(20), ['ps%d' % (6 + hb)])
            S.op('pool', lambda e: e.memset(A(6, 2), 0.0), reads=[], writes=R(6, 2))
            S.op('pool', lambda e: e.memset(A(0, 2), 0.0), reads=[], writes=R(0, 2))
            kvring = [6, 7, 0, 1]
            LTall = A(2, 1)[:, 0:768].rearrange("p (a h) -> p a h", h=8)
            spend = [None]

            def s_back(sb_):
                pt, vres, idx, tok = sb_['pt'], sb_['vres'], sb_['idx'], sb_['tok']
                for hb in range(2):
                    bpv = nb('A')
                    mm(bank(bpv)[0:8, 0:260], pt, A16(9 + (idx % 2))[:, hb * 260:(hb + 1) * 260], True, True,
                       sb_['ptres'] + vres, ['ps%d' % bpv])
                    mk = A16(11)[0:8, hb * 260:(hb + 1) * 260]
                    tt('dve', mk, bank(bpv)[0:8, 0:260], hmask[:, hb * 260:(hb + 1) * 260], ALU.mult,
                       ['ps%d' % bpv, 'hmask'], R(11))
                    mm(OA[hb], sel[:, tok, :], mk, False, False, ['sel'] + R(11), ['ps%d' % (6 + hb)])

            for s4 in range(4):
                for t in range(8):
                    tok = s4 * 8 + t
                    qbc = A(4, 2)[:, 0:1536]
                    ld(qbc, qs_d[tok:tok + 1, :].partition_broadcast(128), R(4, 2), reads=['qsd'], q='sp')
                    for g in range(3):
                        d = DIL[g]
                        L = WIN[g]
                        base = L + t - 128 * d
                        nvalid = 128 - t // d
                        idx = (s4 * 8 + t) * 3 + g
                        kv = A(kvring[idx % 4], 1)
                        kvr = R(kvring[idx % 4])
                        ld(kv[0:nvalid, :], cks[g][li, s4, base:base + (nvalid - 1) * d + 1:d, :], kvr, q='sp')
                        pr = A(8)[:, 0:512]
                        tt('dve', pr, kv[:, 0:512], qbc[:, g * 512:(g + 1) * 512], ALU.mult, kvr + R(4, 2), R(8))
                        S.op('dve', lambda e, o=LTall[:, idx, :], i=pr.rearrange("p (h e) -> p h e", h=8):
                             e.tensor_reduce(out=o, in_=i, axis=mybir.AxisListType.X, op=ALU.add),
                             reads=R(8), writes=R(2))
                        tsl = 3 if idx % 2 == 0 else 19
                        tl = A(tsl)[:, 0:8]
                        tt('dve', tl, LTall[:, idx, :], sbias[:, t * 3 + g, :], ALU.add, R(2) + ['sbias'], R(tsl))
                        pt = A16(tsl)[:, 1024:1032]
                        act(pt, tl, AF.Exp, R(tsl), R(tsl))
                        vb = A16(9 + (idx % 2))[:, 0:520].rearrange("p (h e) -> p h e", h=8)
                        vres = R(9 + (idx % 2))
                        S.op('pool', lambda e, vb=vb: e.memset(vb[:, :, 64:65], 1.0), reads=[], writes=vres)
                        cp('pool', vb[:, :, 0:64], kv[:, 512:1024].rearrange("p (h e) -> p h e", h=8), kvr, vres)
                        cur = dict(pt=pt, ptres=R(tsl), vres=vres, idx=idx, tok=tok)
                        if spend[0] is not None:
                            s_back(spend[0])
                        spend[0] = cur
            s_back(spend[0])
            of = A(12)[0:32, 0:520]
            for hb in range(2):
                mm(OA[hb], zl, zr, False, True, ['a21z'], ['ps%d' % (6 + hb)])
                cp('dve', of[:, hb * 260:(hb + 1) * 260], OA[hb], ['ps%d' % (6 + hb)], R(12))
            ofv = of.rearrange("p (h e) -> p h e", h=8)
            rc = A(13)[0:32, 0:8]
            S.op('dve', lambda e: e.reciprocal(out=rc, in_=ofv[:, :, 64]), reads=R(12), writes=R(13))
            ob = A16(14)[0:32, 0:512]
            tt('dve', ob.rearrange("p (h e) -> p h e", h=8), ofv[:, :, 0:64],
               rc.unsqueeze(2).to_broadcast([32, 8, 64]), ALU.mult, R(12) + R(13), R(14))
            oTs = A16(15)[0:64, 0:256].rearrange("p (h t) -> p h t", h=8)
            bt_ = nb('A')
            for h in range(8):
                tp(bank16(bt_)[0:64, h * 32:(h + 1) * 32], ob[:, h * 64:(h + 1) * 64], ident[0:32, 0:32],
                   R(14) + ['ident'], ['ps%d' % bt_])
            cp('dve', oTs, bank16(bt_)[0:64, 0:256].rearrange("p (h t) -> p h t", h=8), ['ps%d' % bt_], R(15))
            woc = A16(0, 4).rearrange("p (h e) -> p h e", h=8)
            ld(woc[0:64], w_out_c[li].rearrange("(h p) e -> p h e", p=64), R(0, 4))
            gt1, bt1, tab1 = ln_tables(ln1g[l:l + 1, :], ln1b[l:l + 1, :], 1024, 4)
            proj_res_ln1(T, l, lambda i: [oTs[:, h, :] for h in range(8)],
                         lambda j, half: woc[0:64, j, half * 512:(half + 1) * 512],
                         R(15) + R(0, 4), gt1, bt1, tab1)

        def load_x(T, src):
            for i in range(T.ntile):
                ld(T.xr(i), src[i * T.P:(i + 1) * T.P, :], [T.xr_res(i)], q='sp')
                to_xT(T, i, 11)

        def store_y(T, dst):
            for i in range(T.ntile):
                st(dst[i * T.P:(i + 1) * T.P, :], T.xr(i), [T.xr_res(i)])

        for sg in range(nsg if do_prompt else 0):
            with_s = do_sample and sg == 0
            load_x(PR, xp[sg * NT:(sg + 1) * NT, :])
            if with_s:
                load_x(SM, xs)
            for l in range(nlayers):
                if l % 2 == 0:
                    if with_s:
                        ab_layer(SM, l, 0)
                    ab_layer(PR, l, sg)
                else:
                    if with_s:
                        c_layer_sample(l)
                    c_layer_prompt(l, sg)
                mlp([PR, SM] if with_s else [PR], l)
            store_y(PR, yp[sg * NT:(sg + 1) * NT, :])
            if with_s:
                store_y(SM, ys)
        if do_sample and not do_prompt:
            load_x(SM, xs)
            for l in range(nlayers):
                if l % 2 == 0:
                    ab_layer(SM, l, 0)
                else:
                    c_layer_sample(l)
                mlp([SM], l)
            store_y(SM, ys)
        S.finish()
        S.emit()
    return nc


def _consts():
    c = {}
    c["c_ident"] = np.eye(128, dtype=np.float32)
    p = np.arange(128)[:, None]
    q = np.arange(128)[None, :]
    sp = np.where(q <= p, 128 + q - p, BIG).astype(np.float32)
    so = np.where(q >= p, q - p, BIG).astype(np.float32)
    c["c_steps"] = np.concatenate([sp, so], axis=1).astype(np.float32)
    c["c_tril"] = (p <= q).astype(np.float32)
    k = np.arange(32)[:, None]
    qq = np.arange(32)[None, :]
    same = (k // 8) == (qq // 8)
    c["c_bd32"] = (same & ((k % 8) <= (qq % 8))).astype(np.float32)
    sb = np.zeros((128, 24, 8), np.float32)
    for t in range(8):
        for g in range(3):
            d = DIL[g]
            nvalid = 128 - t // d
            for h in range(8):
                v = -SLOPES[h] * d * (128.0 - np.arange(128))
                v[nvalid:] = -BIG
                sb[:, t * 3 + g, h] = v
    c["c_sbias"] = sb.reshape(128, 192)
    ns = np.full((32, 3, 32), BIG, np.float32)
    for kk in range(32):
        for q_ in range(32):
            if kk // 8 != q_ // 8:
                continue
            dt = (q_ % 8) - (kk % 8)
            for g in range(3):
                if dt >= 0 and dt % DIL[g] == 0:
                    ns[kk, g, q_] = dt // DIL[g]
    c["c_nsteps"] = ns.reshape(32, 96)
    hm = np.zeros((8, 8, 65), np.float32)
    for h in range(8):
        hm[h, h, :] = 1.0
    c["c_hmask"] = hm.reshape(8, 520)
    se = np.zeros((8, 32, 32), np.float32)
    for tok in range(32):
        se[:, tok, tok] = 1.0
    c["c_sel"] = se.reshape(8, 1024)
    s65 = np.zeros((65, 64), np.float32)
    s65[64, :] = 1.0
    c["c_sel65"] = s65
    return c


_NC_CACHE = {}


def kernel(x_prompt, x_sample, state_conv, cache_kv_w128, cache_kv_w512, cache_kv_w2048,
           w_in_ab, ln_v_g, ln_v_b, w_spatial, b_spatial, conv_w, w_out_ab,
           w_qkv_c, w_out_c, ln1_g, ln1_b, ln2_g, ln2_b, w_mlp_up, w_mlp_down):
    f = lambda a: np.ascontiguousarray(np.asarray(a, dtype=np.float32))
    if "nc" not in _NC_CACHE:
        _NC_CACHE["nc"] = build_program()
    nc = _NC_CACHE["nc"]
    consts = _consts()
    shared = {
        "w_in_ab": f(w_in_ab), "ln_v_g": f(ln_v_g), "ln_v_b": f(ln_v_b), "w_spatial": f(w_spatial),
        "b_spatial": f(b_spatial), "conv_w": f(conv_w), "w_out_ab": f(w_out_ab), "w_qkv_c": f(w_qkv_c),
        "w_out_c": f(w_out_c), "ln1_g": f(ln1_g), "ln1_b": f(ln1_b), "ln2_g": f(ln2_g), "ln2_b": f(ln2_b),
        "w_mlp_up": f(w_mlp_up), "w_mlp_down": f(w_mlp_down),
    }
    shared.update(consts)
    xpn = f(x_prompt)
    xsn = f(x_sample).reshape(256, D)
    sc = f(state_conv)
    c128 = f(cache_kv_w128).reshape(2, 32, 128, 1024)
    c512 = f(cache_kv_w512).reshape(2, 32, 512, 1024)
    c2048 = f(cache_kv_w2048).reshape(2, 32, 2048, 1024)
    in_maps = []
    for c in range(NCORES):
        m = dict(shared)
        m["xp"] = xpn[c % 2]
        m["xs"] = np.ascontiguousarray(xsn[c * 32:(c + 1) * 32])
        m["sconv"] = np.ascontiguousarray(sc[:, c * 4:(c + 1) * 4])
        m["ck128"] = np.ascontiguousarray(c128[:, c * 4:(c + 1) * 4])
        m["ck512"] = np.ascontiguousarray(c512[:, c * 4:(c + 1) * 4])
        m["ck2048"] = np.ascontiguousarray(c2048[:, c * 4:(c + 1) * 4])
        in_maps.append(m)
    res = run_bass_kernel_spmd(nc, in_maps, core_ids=list(range(NCORES)))
    r = res.results
    y_prompt = np.stack([r[0]["yp"], r[1]["yp"]]).astype(np.float32)
    y_sample = np.concatenate([r[c]["ys"] for c in range(NCORES)], 0).reshape(32, 8, D).astype(np.float32)
    conv_p = np.stack([r[0]["o_convp"], r[1]["o_convp"]], axis=1).astype(np.float32)
    conv_s = np.concatenate([r[c]["o_convs"] for c in range(NCORES)], axis=1).astype(np.float32)
    chunk_v = np.concatenate([r[c]["o_chunkv"] for c in range(NCORES)], axis=1).astype(np.float32)
    outs = [y_prompt, y_sample, conv_p, conv_s, chunk_v]
    for nm, L in (("o_kvp128", 128), ("o_kvp512", 512), ("o_kvp2048", 2048)):
        a = np.stack([r[0][nm], r[1][nm]], axis=1).astype(np.float32)
        outs.append(a.reshape(2, 2, L, 2, 8, 64))
    for nm, L in (("o_kvs128", 128), ("o_kvs512", 512), ("o_kvs2048", 2048)):
        a = np.concatenate([r[c][nm] for c in range(NCORES)], axis=1).astype(np.float32)
        outs.append(a.reshape(2, 32, L, 2, 8, 64))
    return tuple(outs)
```

```python
import numpy as np
from contextlib import ExitStack
import concourse.bass as bass
import concourse.mybir as mybir
from concourse.bass_utils import run_bass_kernel_spmd

F32 = mybir.dt.float32
BF16 = mybir.dt.bfloat16
ALU = mybir.AluOpType
AF = mybir.ActivationFunctionType

D = 1024
SEQ = 8192
NT = 2048
NSG = SEQ // NT
TS = 32
ALPHA = (2.0 * 4) ** 0.25
EPS = 1e-5
BIG = 1.0e30
WIN = (128, 512, 2048)
DIL = (1, 4, 16)
SLOPES = [2.0 ** (-(h + 1)) for h in range(8)]
NCORES = 8
EPOCH = 30000
DBG = set()


class Sched:
    ENG = ['pe', 'act', 'dve', 'pool', 'sp']

    def __init__(self, nc, es, n_dma_sems=40):
        self.nc = nc
        self.es = es
        self.ops = {e: [] for e in self.ENG}
        self.cnt = {e: 0 for e in self.ENG}
        self.epoch = {e: 0 for e in self.ENG}
        self.known = {e: {} for e in self.ENG}
        self.res_w = {}
        self.res_r = {}
        self.sems = {}
        for e in self.ENG:
            self.sems[('e', e, 0)] = es.enter_context(nc.semaphore('s_%s0' % e))
        self.dsem = [es.enter_context(nc.semaphore('d%d' % i)) for i in range(n_dma_sems)]
        self.dcnt = [0] * n_dma_sems
        half = n_dma_sems // 2
        self.dring = {'pool': list(range(0, half)), 'sp': list(range(half, n_dma_sems))}
        self.dnext = {'pool': 0, 'sp': 0}

    def _semof(self, ev):
        if ev[0] == 'd':
            return self.dsem[ev[1]]
        return self.sems[ev[0:3]]

    def _wait(self, engine, ev):
        key = ev[0:3] if ev[0] == 'e' else ev[0:2]
        val = ev[-1]
        if self.known[engine].get(key, 0) >= val:
            return
        self.known[engine][key] = val
        sem = self._semof(ev)
        self.ops[engine].append(lambda eng, sem=sem, val=val: eng.wait_ge(sem, val))

    def _deps(self, engine, reads, writes):
        deps = []
        for r in reads:
            if r in self.res_w:
                deps.append(self.res_w[r])
        for w in writes:
            if w in self.res_w:
                deps.append(self.res_w[w])
            deps.extend(self.res_r.get(w, []))
        for ev in deps:
            if ev[0] == 'e' and ev[1] == engine and engine == 'pe':
                continue
            self._wait(engine, ev)

    def _record(self, ev, reads, writes):
        for r in reads:
            lst = self.res_r.setdefault(r, [])
            if ev[0] == 'e':
                lst[:] = [x for x in lst if not (x[0] == 'e' and x[1] == ev[1] and x[2] == ev[2])]
            lst.append(ev)
        for w in writes:
            self.res_w[w] = ev
            self.res_r[w] = []

    def op(self, engine, fn, reads=(), writes=()):
        self._deps(engine, reads, writes)
        if self.cnt[engine] >= EPOCH:
            self.epoch[engine] += 1
            self.cnt[engine] = 0
            self.sems[('e', engine, self.epoch[engine])] = self.es.enter_context(
                self.nc.semaphore('s_%s%d' % (engine, self.epoch[engine])))
        self.cnt[engine] += 1
        ev = ('e', engine, self.epoch[engine], self.cnt[engine])
        sem = self.sems[ev[0:3]]
        self.ops[engine].append(lambda eng, fn=fn, sem=sem: fn(eng).then_inc(sem, 1))
        self._record(ev, reads, writes)
        return ev

    def dma(self, queue, fn, reads=(), writes=()):
        self._deps(queue, reads, writes)
        ring = self.dring[queue]
        j = ring[self.dnext[queue] % len(ring)]
        self.dnext[queue] += 1
        if self.dcnt[j] > 0:
            self._wait(queue, ('d', j, 16 * self.dcnt[j]))
        self.dcnt[j] += 1
        ev = ('d', j, 16 * self.dcnt[j])
        sem = self.dsem[j]
        self.ops[queue].append(lambda eng, fn=fn, sem=sem: fn(eng).then_inc(sem, 16))
        self._record(ev, reads, writes)
        return ev

    def finish(self):
        for j, c in enumerate(self.dcnt):
            if c > 0:
                self._wait('sp', ('d', j, 16 * c))
        for e in ['pe', 'act', 'dve', 'pool']:
            if self.cnt[e] > 0:
                self._wait('sp', ('e', e, self.epoch[e], self.cnt[e]))

    def emit(self):
        nc = self.nc
        ops = self.ops
        with nc.Block() as block:
            @block.tensor
            def _(e):
                for f in ops['pe']:
                    f(e)

            @block.scalar
            def _(e):
                for f in ops['act']:
                    f(e)

            @block.vector
            def _(e):
                for f in ops['dve']:
                    f(e)

            @block.gpsimd
            def _(e):
                for f in ops['pool']:
                    f(e)

            @block.sync
            def _(e):
                for f in ops['sp']:
                    f(e)


def build_program(nsg=NSG, do_sample=True, do_prompt=True, nlayers=4, copy_cache=True):
    nc = bass.Bass("TRN2", target_bir_lowering=False)

    def din(name, shape, dt=F32):
        return nc.dram_tensor(name, list(shape), dt, kind="ExternalInput").ap()

    def dout(name, shape):
        return nc.dram_tensor(name, list(shape), F32, kind="ExternalOutput").ap()

    xp = din("xp", [SEQ, D])
    xs = din("xs", [TS, D])
    sconv = din("sconv", [2, 4, 2, 512])
    cks = [din("ck128", [2, 4, 128, 1024]), din("ck512", [2, 4, 512, 1024]), din("ck2048", [2, 4, 2048, 1024])]
    w_in = din("w_in_ab", [2, D, 2560])
    lnv_g = din("ln_v_g", [2, 512])
    lnv_b = din("ln_v_b", [2, 512])
    w_sp = din("w_spatial", [2, 4, 128, 128])
    b_sp = din("b_spatial", [2, 4, 128])
    conv_w = din("conv_w", [2, 3, 512])
    w_out_ab = din("w_out_ab", [2, D, D])
    w_qkv = din("w_qkv_c", [2, D, 4608])
    w_out_c = din("w_out_c", [2, 512, D])
    ln1g = din("ln1_g", [4, D])
    ln1b = din("ln1_b", [4, D])
    ln2g = din("ln2_g", [4, D])
    ln2b = din("ln2_b", [4, D])
    w_up = din("w_mlp_up", [4, D, 4096])
    w_dn = din("w_mlp_down", [4, 4096, D])
    c_ident = din("c_ident", [128, 128])
    c_steps = din("c_steps", [128, 256])
    c_tril = din("c_tril", [128, 128])
    c_bd32 = din("c_bd32", [32, 32])
    c_sbias = din("c_sbias", [128, 24 * 8])
    c_nsteps = din("c_nsteps", [32, 3 * 32])
    c_hmask = din("c_hmask", [8, 520])
    c_sel = din("c_sel", [8, 32 * 32])
    c_sel65 = din("c_sel65", [65, 64])

    yp = dout("yp", [SEQ, D])
    ys = dout("ys", [TS, D])
    o_convp = dout("o_convp", [2, 2, 512])
    o_convs = dout("o_convs", [2, 4, 2, 512])
    o_chunkv = dout("o_chunkv", [2, 4, 8, 512])
    o_kvp = [dout("o_kvp128", [2, 128, 1024]), dout("o_kvp512", [2, 512, 1024]), dout("o_kvp2048", [2, 2048, 1024])]
    o_kvs = [dout("o_kvs128", [2, 4, 128, 1024]), dout("o_kvs512", [2, 4, 512, 1024]),
             dout("o_kvs2048", [2, 4, 2048, 1024])]

    kT_d = nc.dram_tensor("kT_d", [2, 2, 3, 4, 128, 16, 128], BF16).ap()
    v_d = nc.dram_tensor("v_d", [2, 2, 3, 4, 128, 2080], BF16).ap()
    qs_d = nc.dram_tensor("qs_d", [TS, 1536], F32).ap()

    es = ExitStack()
    with es:
        S = Sched(nc, es)
        sbt = lambda name, shape, dt: es.enter_context(nc.sbuf_tensor(name, shape, dt))
        xres = sbt("xres", [128, 16, D], F32)
        xT = sbt("xT", [128, 8, NT], BF16)
        ar = sbt("arena", [128, 24 * 1024], F32)
        xres_s = sbt("xres_s", [TS, D], F32)
        xT_s = sbt("xT_s", [128, 8, TS], BF16)
        carry = sbt("carry", [128, 2, 4, 2], F32)
        cin_s = sbt("cin_s", [128, 4, 4, 10], F32)
        ident = sbt("ident", [128, 128], BF16)
        steps = sbt("steps", [128, 256], F32)
        tril = sbt("tril", [128, 128], F32)
        bd32 = sbt("bd32", [32, 32], F32)
        sbias = sbt("sbias", [128, 24, 8], F32)
        nsteps = sbt("nsteps", [32, 3, 32], F32)
        hmask = sbt("hmask", [8, 520], F32)
        sel = sbt("sel", [8, 32, 32], BF16)
        sel65 = sbt("sel65", [65, 64], F32)
        stat = sbt("stat", [128, 32], F32)
        stat6 = sbt("stat6", [128, 16, 12], F32)
        mvs = sbt("mvs", [128, 16, 2], F32)
        rstds = sbt("rstds", [128, 16], F32)
        nbias = sbt("nbias", [128, 16], F32)
        cw = sbt("cw", [128, 4, 3], F32)
        ps = es.enter_context(nc.psum_tensor("ps", [128, 4096], F32))

        def A(k, n=1):
            return ar[:, k * 1024:(k + n) * 1024]

        def A16(k, n=1):
            return ar[:, k * 1024:(k + n) * 1024].bitcast(BF16)

        def R(k, n=1):
            return ['a%d' % i for i in range(k, k + n)]

        def bank(b):
            return ps[:, b * 512:(b + 1) * 512]

        def bank16(b):
            return ps[:, b * 512:(b + 1) * 512].bitcast(BF16)

        bank_rr = {'A': [0, 1], 'B': [2, 3], 'C': [4, 5], 'D': [6, 7]}
        bank_i = {'A': 0, 'B': 0, 'C': 0, 'D': 0}

        def nb(kind):
            lst = bank_rr[kind]
            b = lst[bank_i[kind] % len(lst)]
            bank_i[kind] += 1
            return b

        def ld(out, in_, writes, reads=(), q='pool'):
            S.dma(q, lambda e, out=out, in_=in_: e.dma_start(out=out, in_=in_, allow_slow_non_contiguous=True), reads=reads, writes=writes)

        def st(out, in_, reads, writes=()):
            S.dma('sp', lambda e, out=out, in_=in_: e.dma_start(out=out, in_=in_, allow_slow_non_contiguous=True), reads=reads, writes=writes)

        def mm(out, lhsT, rhs, start, stop, reads, writes):
            S.op('pe', lambda e, out=out, lhsT=lhsT, rhs=rhs, start=start, stop=stop:
                 e.matmul(out, lhsT, rhs, start=start, stop=stop), reads=reads, writes=writes)

        def tp(out, in_, idn, reads, writes):
            S.op('pe', lambda e, out=out, in_=in_, idn=idn: e.transpose(out, in_, idn), reads=reads, writes=writes)

        def act(out, in_, func, reads, writes, scale=1.0):
            S.op('act', lambda e, out=out, in_=in_, func=func, scale=scale:
                 e.activation(out=out, in_=in_, func=func, scale=scale), reads=reads, writes=writes)

        def tt(eng, out, in0, in1, op, reads, writes):
            S.op(eng, lambda e, out=out, in0=in0, in1=in1, op=op: e.tensor_tensor(out=out, in0=in0, in1=in1, op=op),
                 reads=reads, writes=writes)

        def ts(eng, out, in0, s1, s2, op0, op1, reads, writes):
            if s2 is None:
                S.op(eng, lambda e, out=out, in0=in0, s1=s1, op0=op0:
                     e.tensor_scalar(out=out, in0=in0, scalar1=s1, scalar2=None, op0=op0), reads=reads, writes=writes)
            else:
                S.op(eng, lambda e, out=out, in0=in0, s1=s1, s2=s2, op0=op0, op1=op1:
                     e.tensor_scalar(out=out, in0=in0, scalar1=s1, scalar2=s2, op0=op0, op1=op1),
                     reads=reads, writes=writes)

        def stt(eng, out, in0, scalar, in1, op0, op1, reads, writes):
            S.op(eng, lambda e, out=out, in0=in0, scalar=scalar, in1=in1, op0=op0, op1=op1:
                 e.scalar_tensor_tensor(out=out, in0=in0, scalar=scalar, in1=in1, op0=op0, op1=op1),
                 reads=reads, writes=writes)

        def cp(eng, out, in_, reads, writes):
            if eng == 'act':
                S.op(eng, lambda e, out=out, in_=in_: e.activation(out=out, in_=in_, func=AF.Copy),
                     reads=reads, writes=writes)
            else:
                S.op(eng, lambda e, out=out, in_=in_: e.tensor_copy(out=out, in_=in_), reads=reads, writes=writes)

        ld(ident[:], c_ident, ['ident'])
        ld(steps[:], c_steps, ['steps'], q='sp')
        ld(tril[:], c_tril, ['tril'], q='sp')
        ld(bd32[:], c_bd32, ['bd32'], q='sp')
        ld(sbias[:].rearrange("p a b -> p (a b)"), c_sbias, ['sbias'], q='sp')
        ld(nsteps[:].rearrange("p a b -> p (a b)"), c_nsteps, ['nsteps'], q='sp')
        ld(hmask[:], c_hmask, ['hmask'], q='sp')
        ld(sel[:].rearrange("p a b -> p (a b)"), c_sel, ['sel'])
        ld(sel65[:], c_sel65, ['sel65'], q='sp')

        for i in range(2 if copy_cache else 0):
            for g in range(3):
                L = WIN[g]
                for s4 in range(4):
                    st(o_kvs[g][i, s4, 0:L - 8, :], cks[g][i, s4, 8:L, :], reads=[])

        class TSet:
            pass

        def mk_prompt():
            t = TSet()
            t.P = 128
            t.ntile = 16
            t.ntok = NT
            t.xr = lambda i: xres[:, i, :]
            t.xr_res = lambda i: 'xr%d' % i
            t.xT = xT
            t.xT_res = lambda tg: 'xT%d' % tg
            t.groups = [(tg * 512, 512) for tg in range(4)]
            t.sample = False
            return t

        def mk_sample():
            t = TSet()
            t.P = TS
            t.ntile = 1
            t.ntok = TS
            t.xr = lambda i: xres_s[:, :]
            t.xr_res = lambda i: 'xrs'
            t.xT = xT_s
            t.xT_res = lambda tg: 'xTs'
            t.groups = [(0, TS)]
            t.sample = True
            return t

        PR = mk_prompt()
        SM = mk_sample()

        def tile_group(T, i):
            return (i * T.P) // 512

        def layernorm_multi(items, P, W, gt, bt, tabres):
            nch = W // 512
            n = len(items)
            for j, (xap, res) in enumerate(items):
                sv = stat6[0:P, j, 0:6 * nch].rearrange("p (c s) -> p c s", c=nch)
                for c in range(nch):
                    S.op('dve', lambda e, o=sv[:, c, :], i=xap[:, c * 512:(c + 1) * 512]: e.bn_stats(out=o, in_=i),
                         reads=[res], writes=['stat'])
                S.op('dve', lambda e, o=mvs[0:P, j, :], i=sv: e.bn_aggr(out=o, in_=i), reads=['stat'], writes=['stat'])
            ts('dve', rstds[0:P, 0:n], mvs[0:P, 0:n, 1], EPS, None, ALU.add, None, ['stat'], ['stat'])
            act(rstds[0:P, 0:n], rstds[0:P, 0:n], AF.Sqrt, ['stat'], ['stat'])
            S.op('dve', lambda e, o=rstds[0:P, 0:n]: e.reciprocal(out=o, in_=o), reads=['stat'], writes=['stat'])
            stt('dve', nbias[0:P, 0:n], mvs[0:P, 0:n, 0], -1.0, rstds[0:P, 0:n], ALU.mult, ALU.mult, ['stat'], ['stat'])
            for j, (xap, res) in enumerate(items):
                S.op('act', lambda e, o=xap, b_=nbias[0:P, j:j + 1], sc=rstds[0:P, j:j + 1]:
                     e.activation(out=o, in_=o, func=AF.Identity, bias=b_, scale=sc), reads=[res, 'stat'], writes=[res])
                tt('dve', xap, xap, gt, ALU.mult, [res] + tabres, [res])
                tt('dve', xap, xap, bt, ALU.add, [res] + tabres, [res])

        def layernorm(xap, P, W, gt, bt, res, tabres):
            layernorm_multi([(xap, res)], P, W, gt, bt, tabres)

        def ln_tables(gsrc, bsrc, W, slot):
            if W == 1024:
                gt, bt = A(slot), A(slot + 1)
                ld(gt, gsrc.partition_broadcast(128), R(slot), q='sp')
                ld(bt, bsrc.partition_broadcast(128), R(slot + 1), q='sp')
                return gt, bt, R(slot, 2)
            gt, bt = A(slot)[:, 0:512], A(slot)[:, 512:1024]
            ld(gt, gsrc.partition_broadcast(128), R(slot), q='sp')
            ld(bt, bsrc.partition_broadcast(128), R(slot), q='sp')
            return gt, bt, R(slot)

        def to_xT(T, i, xbslot):
            P = T.P
            xb = A16(xbslot)[0:P, 0:1024]
            act(xb, T.xr(i), AF.Copy, [T.xr_res(i)], R(xbslot))
            b = nb('D')
            pb = bank16(b)
            for kc in range(8):
                tp(pb[:, kc * P:(kc + 1) * P], xb[:, kc * 128:(kc + 1) * 128], ident[0:P, 0:P],
                   R(xbslot) + ['ident'], ['ps%d' % b])
            c0 = i * P
            cp('dve', T.xT[:, :, c0:c0 + P], pb[:, 0:8 * P].rearrange("p (k t) -> p k t", k=8),
               ['ps%d' % b], [T.xT_res(tile_group(T, i))])

        def mlp(Ts, l):
            gt, bt, tabres = ln_tables(ln2g[l:l + 1, :], ln2b[l:l + 1, :], 1024, 18)
            bank_rr['B'] = [2, 3, 4, 5]
            NFC = 4
            for fb in range(8):
                par = fb % 2
                wup = A16(0 + 2 * par, 2).rearrange("p (k f) -> p k f", k=8)
                wdn = A16(4 + 2 * par, 2).rearrange("p (c d) -> p c d", c=NFC)
                wur, wdr = R(0 + 2 * par, 2), R(4 + 2 * par, 2)
                ld(wup, w_up[l, :, fb * 512:(fb + 1) * 512].rearrange("(k p) f -> p k f", p=128), wur)
                ld(wdn, w_dn[l, fb * 512:(fb + 1) * 512, :].rearrange("(c p) d -> p c d", p=128), wdr)
                for T in Ts:
                    P = T.P
                    if T.sample:
                        hT = A16(23)[:, 0:128].rearrange("p (c t) -> p c t", c=NFC)
                        hres = R(23)
                    else:
                        hT = A16(8 + 4 * par, 4).rearrange("p (c t) -> p c t", c=NFC)
                        hres = R(8 + 4 * par, 4)
                    for gi, (c0, w) in enumerate(T.groups):
                        for fc in range(NFC):
                            b = nb('A')
                            for kc in range(8):
                                mm(bank(b)[:, 0:w], wup[:, kc, fc * 128:(fc + 1) * 128], T.xT[:, kc, c0:c0 + w],
                                   kc == 0, kc == 7, wur + [T.xT_res(gi)], ['ps%d' % b])
                            tslot = 16 if (bank_i['A'] % 2) else 17
                            tmp = A(tslot)[:, 0:512]
                            act(tmp[:, 0:w], bank(b)[:, 0:w], AF.Relu, ['ps%d' % b], R(tslot))
                            act(hT[:, fc, c0:c0 + w], tmp[:, 0:w], AF.Square, R(tslot), hres)
                    for i in range(T.ntile):
                        c0 = i * P
                        for half in range(2):
                            b = nb('B')
                            for fc in range(NFC):
                                mm(bank(b)[0:P, :], hT[:, fc, c0:c0 + P], wdn[:, fc, half * 512:(half + 1) * 512],
                                   fc == 0, fc == NFC - 1, hres + wdr, ['ps%d' % b])
                            xa = T.xr(i)[:, half * 512:(half + 1) * 512]
                            if fb == 0:
                                stt('dve', xa, xa, ALPHA, bank(b)[0:P, :], ALU.mult, ALU.add,
                                    [T.xr_res(i), 'ps%d' % b], [T.xr_res(i)])
                            else:
                                tt('dve', xa, xa, bank(b)[0:P, :], ALU.add, [T.xr_res(i), 'ps%d' % b], [T.xr_res(i)])
                        if fb == 7 and (i % 4 == 3 or i == T.ntile - 1):
                            batch = list(range(i - (i % 4), i + 1))
                            layernorm_multi([(T.xr(j), T.xr_res(j)) for j in batch], P, 1024, gt[0:P, :], bt[0:P, :],
                                            tabres)
                            for j in batch:
                                to_xT(T, j, 20)
            bank_rr['B'] = [2, 3]

        def proj_res_ln1(T, l, lhs_list, rhs_fn, rd, gt, bt, tabres, xbslot=6):
            P = T.P
            for i in range(T.ntile):
                lh = lhs_list(i)
                for half in range(2):
                    b = nb('B')
                    for j, l_ap in enumerate(lh):
                        mm(bank(b)[0:P, :], l_ap, rhs_fn(j, half), j == 0, j == len(lh) - 1, rd, ['ps%d' % b])
                    xa = T.xr(i)[:, half * 512:(half + 1) * 512]
                    stt('dve', xa, xa, ALPHA, bank(b)[0:P, :], ALU.mult, ALU.add,
                        [T.xr_res(i), 'ps%d' % b], [T.xr_res(i)])
                if i % 4 == 3 or i == T.ntile - 1:
                    batch = list(range(i - (i % 4), i + 1))
                    layernorm_multi([(T.xr(j), T.xr_res(j)) for j in batch], P, 1024, gt[0:P, :], bt[0:P, :], tabres)
                    for j in batch:
                        to_xT(T, j, xbslot)

        def ab_layer(T, l, sg):
            li = l // 2
            P = T.P
            win = A16(13, 10).rearrange("p (k e) -> p k e", k=8)
            for kc in range(8):
                ld(win[:, kc, :], w_in[li, kc * 128:(kc + 1) * 128, :], R(13, 10))
            wout = A16(4, 4).rearrange("p (k e) -> p k e", k=8)
            wsT = A16(3)[:, 0:512].rearrange("p (g i) -> p g i", g=4)
            wtmp = A(3)[:, 256:768].rearrange("p (g i) -> p g i", g=4)
            wraw = A(3)[:, 768:1024].bitcast(BF16).rearrange("p (g i) -> p g i", g=4)
            gv, bv, tabv = ln_tables(lnv_g[li:li + 1, :], lnv_b[li:li + 1, :], 512, 0)
            bsf = A(1)[:, 0:512].rearrange("p (g c) -> p g c", g=4)
            bst = stat[:, 24:28]
            if not T.sample:
                ld(wraw, w_sp[li].rearrange("g i j -> i g j"), R(3))
                b = nb('D')
                for g in range(4):
                    tp(bank16(b)[:, g * 128:(g + 1) * 128], wraw[:, g, :], ident[:], R(3) + ['ident'], ['ps%d' % b])
                tt('dve', wsT, bank16(b)[:, 0:512].rearrange("p (g i) -> p g i", g=4),
                   tril[:].unsqueeze(1).to_broadcast([128, 4, 128]), ALU.mult, ['ps%d' % b, 'tril'], R(3))
                with nc.allow_non_contiguous_dma(reason="tiny transposed bias load"):
                    ld(bst, b_sp[li].rearrange("g i -> i g"), ['bst'], q='sp')
                cp('pool', bsf, bst.unsqueeze(2).to_broadcast([128, 4, 128]), ['bst'], R(1))
            else:
                S.op('pool', lambda e: e.memset(wtmp[0:32, :, 0:32], 0.0), reads=[], writes=R(3))
                with nc.allow_non_contiguous_dma(reason="tiny transposed loads"):
                    for s4 in range(4):
                        for g in range(4):
                            ld(wtmp[s4 * 8:(s4 + 1) * 8, g, s4 * 8:(s4 + 1) * 8],
                               w_sp[li, g, 0:8, 0:8].rearrange("i j -> j i"), R(3), q='sp')
                        ld(bst[s4 * 8:(s4 + 1) * 8, :], b_sp[li, :, 0:8].rearrange("g i -> i g"), ['bst'], q='sp')
                tt('dve', wsT[0:32, :, 0:32], wtmp[0:32, :, 0:32],
                   bd32[:].unsqueeze(1).to_broadcast([32, 4, 32]), ALU.mult, R(3) + ['bd32'], R(3))
                cp('pool', bsf[0:32], bst[0:32].unsqueeze(2).to_broadcast([32, 4, 128]), ['bst'], R(1))
            with nc.allow_non_contiguous_dma(reason="tiny transposed conv weight load"):
                for k in range(3):
                    ld(cw[:, :, k], conv_w[li, k].rearrange("(c p) -> p c", p=128), ['cw'], q='sp')
            gt1, bt1, tab1 = ln_tables(ln1g[l:l + 1, :], ln1b[l:l + 1, :], 1024, 9)

            if T.sample:
                with nc.allow_non_contiguous_dma(reason="tiny transposed conv state load"):
                    for s4 in range(4):
                        for t_ in range(2):
                            ld(cin_s[:, :, s4, t_], sconv[li, s4, t_].rearrange("(c p) -> p c", p=128), ['cin'], q='sp')
            elif sg == 0:
                S.op('pool', lambda e, li=li: e.memset(carry[:, li], 0.0), reads=[], writes=['carry'])

            catT = A16(11, 2).rearrange("p (k t) -> p k t", k=8)
            for gi, (c0, w) in enumerate(T.groups):
                ntl = max(1, w // P)
                def a_stage1(ti):
                    i = gi * 4 + ti if not T.sample else 0
                    tc0 = c0 + ti * P
                    bu, bvk = nb('A'), nb('A')
                    for (bb, col) in ((bu, 0), (bvk, 512)):
                        for kc in range(8):
                            mm(bank(bb)[0:P, :], T.xT[:, kc, tc0:tc0 + P], win[:, kc, col:col + 512],
                               kc == 0, kc == 7, [T.xT_res(gi)] + R(13, 10), ['ps%d' % bb])
                    xs_ = 4 + (ti % 2)
                    xz = A(xs_)[0:P, :]
                    act(xz[:, 0:512], bank(bu)[0:P, :], AF.Gelu_apprx_tanh, ['ps%d' % bu], R(xs_))
                    act(xz[:, 512:1024], bank(bvk)[0:P, :], AF.Gelu_apprx_tanh, ['ps%d' % bvk], R(xs_))
                    vv = xz[:, 512:1024]
                    layernorm(vv, P, 512, gv[0:P, :], bv[0:P, :], 'a%d' % xs_, tabv)
                    if T.sample:
                        st(o_chunkv[li].rearrange("s t c -> (s t) c"), vv, R(xs_))
                    vb = A16(7)[0:P, (ti % 2) * 512:(ti % 2 + 1) * 512]
                    act(vb, vv, AF.Copy, R(xs_), R(7))
                    return dict(ti=ti, xs_=xs_, xz=xz, vb=vb)

                def a_stage2(sa):
                    ti, xs_, xz, vb = sa['ti'], sa['xs_'], sa['xz'], sa['vb']
                    bsp = nb('C')
                    for g in range(4):
                        mm(bank(bsp)[0:P, g * 128:(g + 1) * 128], wsT[0:P, g, 0:P], vb[:, g * 128:(g + 1) * 128],
                           True, True, R(3) + R(7), ['ps%d' % bsp])
                    t3 = A(6)[0:P, 0:512]
                    tt('dve', t3, bank(bsp)[0:P, :], A(1)[0:P, 0:512], ALU.add, ['ps%d' % bsp] + R(1), R(6))
                    ao = A16(6)[0:P, 1024:1536]
                    tt('dve', ao, t3, xz[:, 0:512], ALU.mult, R(6) + R(xs_), R(6))
                    bt_ = nb('D')
                    for ec in range(4):
                        tp(bank16(bt_)[:, ec * P:(ec + 1) * P], ao[:, ec * 128:(ec + 1) * 128], ident[0:P, 0:P],
                           R(6) + ['ident'], ['ps%d' % bt_])
                    cp('act', catT[:, 0:4, ti * P:(ti + 1) * P],
                       bank16(bt_)[:, 0:4 * P].rearrange("p (k t) -> p k t", k=4), ['ps%d' % bt_], R(11, 2))

                pend = None
                for ti in range(ntl):
                    cur = a_stage1(ti)
                    if pend is not None:
                        a_stage2(pend)
                    pend = cur
                a_stage2(pend)
                ld(wout, w_out_ab[li].rearrange("(k p) e -> p k e", p=128), R(4, 4))
                for ec in range(4):
                    bbg, bcg, bh = nb('B'), nb('A'), nb('C')
                    for (bb, col) in ((bbg, 1024), (bcg, 1536), (bh, 2048)):
                        for kc in range(8):
                            mm(bank(bb)[:, 0:w], win[:, kc, col + ec * 128:col + (ec + 1) * 128],
                               T.xT[:, kc, c0:c0 + w], kc == 0, kc == 7, [T.xT_res(gi)] + R(13, 10), ['ps%d' % bb])
                    zh = A(2)[:, 512:1024]
                    act(zh[:, 0:w], bank(bh)[:, 0:w], AF.Copy, ['ps%d' % bh], R(2))
                    co = A(8)[:, 0:512]
                    if not T.sample:
                        ci = A(23)[:, 0:514]
                        cp('act', ci[:, 0:2], carry[:, li, ec, :], ['carry'], ['cin'])
                        tt('dve', ci[:, 2:2 + w], bank(bcg)[:, 0:w], zh[:, 0:w], ALU.mult, ['ps%d' % bcg, 'a2'], ['cin'])
                        ts('dve', co[:, 0:w], ci[:, 2:2 + w], cw[:, ec, 2:3], None, ALU.mult, None, ['cin', 'cw'], R(8))
                        stt('dve', co[:, 0:w], ci[:, 1:1 + w], cw[:, ec, 1:2], co[:, 0:w], ALU.mult, ALU.add,
                            ['cin', 'cw', 'a8'], R(8))
                        stt('dve', co[:, 0:w], ci[:, 0:w], cw[:, ec, 0:1], co[:, 0:w], ALU.mult, ALU.add,
                            ['cin', 'cw', 'a8'], R(8))
                        tt('dve', catT[:, 4 + ec, 0:w], bank(bbg)[:, 0:w], co[:, 0:w], ALU.mult,
                           ['ps%d' % bbg, 'a8'], R(11, 2))
                        if gi == 3 and sg == NSG - 1:
                            with nc.allow_non_contiguous_dma(reason="tiny transposed conv state store"):
                                st(o_convp[li].rearrange("t (c p) -> p c t", p=128)[:, ec, :], ci[:, 512:514], ['cin'])
                        cp('act', carry[:, li, ec, :], ci[:, 512:514], ['cin'], ['carry'])
                    else:
                        ci = cin_s[:, ec, :, :]
                        co3 = co[:, 0:32].rearrange("p (s t) -> p s t", s=4)
                        tt('dve', ci[:, :, 2:10], bank(bcg)[:, 0:32].rearrange("p (s t) -> p s t", s=4),
                           zh[:, 0:32].rearrange("p (s t) -> p s t", s=4), ALU.mult, ['ps%d' % bcg, 'a2'], ['cin'])
                        ts('dve', co3, ci[:, :, 2:10], cw[:, ec, 2:3], None, ALU.mult, None, ['cin', 'cw'], R(8))
                        stt('dve', co3, ci[:, :, 1:9], cw[:, ec, 1:2], co3, ALU.mult, ALU.add,
                            ['cin', 'cw', 'a8'], R(8))
                        stt('dve', co3, ci[:, :, 0:8], cw[:, ec, 0:1], co3, ALU.mult, ALU.add,
                            ['cin', 'cw', 'a8'], R(8))
                        tt('dve', catT[:, 4 + ec, 0:w], bank(bbg)[:, 0:w], co[:, 0:w], ALU.mult,
                           ['ps%d' % bbg, 'a8'], R(11, 2))
                        with nc.allow_non_contiguous_dma(reason="tiny transposed conv state store"):
                            for s4 in range(4):
                                st(o_convs[li, s4].rearrange("t (c p) -> p c t", p=128)[:, ec, :], ci[:, s4, 8:10],
                                   ['cin'])
                for ti in range(ntl):
                    i = gi * 4 + ti if not T.sample else 0
                    for half in range(2):
                        b = nb('B')
                        for k in range(8):
                            mm(bank(b)[0:P, :], catT[:, k, ti * P:(ti + 1) * P], wout[:, k, half * 512:(half + 1) * 512],
                               k == 0, k == 7, R(11, 2) + R(4, 4), ['ps%d' % b])
                        xa = T.xr(i)[:, half * 512:(half + 1) * 512]
                        stt('dve', xa, xa, ALPHA, bank(b)[0:P, :], ALU.mult, ALU.add,
                            [T.xr_res(i), 'ps%d' % b], [T.xr_res(i)])
                gtiles = [(gi * 4 + ti if not T.sample else 0) for ti in range(ntl)]
                layernorm_multi([(T.xr(i), T.xr_res(i)) for i in gtiles], P, 1024, gt1[0:P, :], bt1[0:P, :], tab1)
                for ti in range(ntl):
                    i = gi * 4 + ti if not T.sample else 0
                    to_xT(T, i, 2)

        def blk_cols(g, blk):
            d = DIL[g]
            nper = 16 // d
            r, n = blk // nper, blk % nper
            return r, n, nper

        def c_layer_prompt(l, sg):
            li = l // 2
            T = PR
            par = sg % 2
            if sg == NSG - 1:
                for g in range(3):
                    ntk = WIN[g] // 128
                    wk = A16(0, 2).rearrange("p (k e) -> p k e", k=8)
                    wv = A16(2, 2).rearrange("p (k e) -> p k e", k=8)
                    ld(wk, w_qkv[li, :, 1536 + g * 512:1536 + (g + 1) * 512].rearrange("(k p) e -> p k e", p=128), R(0, 2))
                    ld(wv, w_qkv[li, :, 3072 + g * 512:3072 + (g + 1) * 512].rearrange("(k p) e -> p k e", p=128), R(2, 2))
                    for i in range(16 - ntk, 16):
                        c0 = i * 128
                        ob = A(4 + (i % 2))
                        for (wsrc, wr, off) in ((wk, R(0, 2), 0), (wv, R(2, 2), 512)):
                            b = nb('A')
                            for kc in range(8):
                                mm(bank(b), xT[:, kc, c0:c0 + 128], wsrc[:, kc, :], kc == 0, kc == 7,
                                   ['xT%d' % (i // 4)] + wr, ['ps%d' % b])
                            act(ob[:, off:off + 512], bank(b), AF.Copy, ['ps%d' % b], R(4 + (i % 2)))
                        row0 = (i - (16 - ntk)) * 128
                        st(o_kvp[g][li, row0:row0 + 128, :], ob, R(4 + (i % 2)))
            oT = A16(16, 8).rearrange("p (h t) -> p h t", h=8)
            for hp in range(4):
                acc = A(12, 4).rearrange("p (h t) -> p h t", h=2)
                for g in range(3):
                    d = DIL[g]
                    nper = 16 // d
                    M = NT // d
                    cscale = [-SLOPES[2 * hp + hh] * d for hh in range(2)]
                    wq = A16(0)[:, 0:1024].rearrange("p (k e) -> p k e", k=8)
                    wk = A16(0)[:, 1024:2048].rearrange("p (k e) -> p k e", k=8)
                    wv = A16(1)[:, 0:1024].rearrange("p (k e) -> p k e", k=8)
                    col = g * 512 + hp * 128
                    ld(wq, w_qkv[li, :, col:col + 128].rearrange("(k p) e -> p k e", p=128), R(0))
                    ld(wk, w_qkv[li, :, 1536 + col:1536 + col + 128].rearrange("(k p) e -> p k e", p=128), R(0))
                    ld(wv, w_qkv[li, :, 3072 + col:3072 + col + 128].rearrange("(k p) e -> p k e", p=128), R(1))
                    QT = A16(2)
                    KT = A16(3)
                    V = A16(4, 2)[:, 0:16 * 130].rearrange("p (b h e) -> p b h e", b=16, h=2)
                    S.op('pool', lambda e, V=V: e.memset(V[:, :, :, 64:65], 1.0), reads=[], writes=R(4, 2))
                    QT1 = A16(11)
                    S.op('pool', lambda e, QT=QT: e.memset(QT[64:128, :], 0.0), reads=[], writes=R(2))
                    S.op('pool', lambda e, QT1=QT1: e.memset(QT1[0:64, :], 0.0), reads=[], writes=R(11))
                    QTm = [QT, QT1]
                    for which in range(2):
                        wsrc = wq if which == 0 else wk
                        for tg in range(4):
                            b = nb('A')
                            for kc in range(8):
                                mm(bank(b), wsrc[:, kc, :], xT[:, kc, tg * 512:(tg + 1) * 512], kc == 0, kc == 7,
                                   ['a0', 'xT%d' % tg], ['ps%d' % b])
                            if which == 1:
                                cp('dve', KT[:, tg * 512:(tg + 1) * 512], bank(b), ['ps%d' % b], ['a3'])
                            else:
                                for hh in range(2):
                                    pp = slice(hh * 64, (hh + 1) * 64)
                                    act(QTm[hh][pp, tg * 512:(tg + 1) * 512], bank(b)[pp, :], AF.Copy, ['ps%d' % b],
                                        R(2) if hh == 0 else R(11), scale=0.125)
                    for blk in range(0 if 'noV' in DBG else 16):
                        r, n, _ = blk_cols(g, blk)
                        b = nb('C')
                        start_tok = n * 128 * d + r
                        for kc in range(8):
                            lcols = xT[:, kc, start_tok:start_tok + 127 * d + 1:d]
                            mm(bank(b)[:, 0:128], lcols, wv[:, kc, :], kc == 0, kc == 7,
                               R(1) + ['xT%d' % t_ for t_ in range(4)], ['ps%d' % b])
                        cp('dve', V[:, blk, :, 0:64], bank(b)[:, 0:128].rearrange("p (h e) -> p h e", h=2),
                           ['ps%d' % b], R(4, 2))
                    if sg < NSG - 1 and 'noStore' not in DBG:
                        st(kT_d[li, par, g, hp].rearrange("p b k -> p (b k)"), KT, ['a3'], ['kTd%d' % par])
                        st(v_d[li, par, g, hp], A16(4, 2)[:, 0:2080], R(4, 2), ['vd%d' % par])
                    KTh = A16(6)
                    Vh = A16(7, 2)[:, 0:16 * 130].rearrange("p (b h e) -> p b h e", b=16, h=2)
                    if sg > 0:
                        ld(KTh, kT_d[li, 1 - par, g, hp].rearrange("p b k -> p (b k)"), R(6),
                           reads=['kTd%d' % (1 - par)], q='sp')
                        ld(A16(7, 2)[:, 0:2080], v_d[li, 1 - par, g, hp], R(7, 2), reads=['vd%d' % (1 - par)], q='sp')
                    def att_front(blk):
                        r, n, _ = blk_cols(g, blk)
                        tok0 = n * 128 * d + r
                        span = 127 * d + 1
                        st8 = dict(blk=blk, r=r, n=n, tok0=tok0)
                        if n >= 1:
                            kprev = KT[:, tok0 - 128 * d:tok0 - 128 * d + span:d]
                            st8['vprev'] = V[:, blk - 1]
                            st8['pres'] = ['a3'] + R(4, 2)
                            has_prev = True
                        elif sg > 0:
                            pb_ = r * nper + nper - 1
                            ptok0 = (nper - 1) * 128 * d + r
                            kprev = KTh[:, ptok0:ptok0 + span:d]
                            st8['vprev'] = Vh[:, pb_]
                            st8['pres'] = R(6) + R(7, 2)
                            has_prev = True
                        else:
                            has_prev = False
                        st8['has_prev'] = has_prev
                        kown = KT[:, tok0:tok0 + span:d]
                        bl = nb('B')
                        LT = bank(bl).rearrange("p (h c q) -> p h c q", h=2, c=2)
                        for hh in range(2):
                            qb = QTm[hh][:, tok0:tok0 + span:d]
                            qres = R(2) if hh == 0 else R(11)
                            if has_prev:
                                mm(LT[:, hh, 0, :], kprev, qb, True, True, st8['pres'] + qres, ['ps%d' % bl])
                            mm(LT[:, hh, 1, :], kown, qb, True, True, ['a3'] + qres, ['ps%d' % bl])
                        c_lo = 0 if has_prev else 1
                        tl = A(9 + (blk % 2))[:, 0:512].rearrange("p (h c q) -> p h c q", h=2, c=2)
                        tlr = R(9 + (blk % 2))
                        PTt = A16(9 + (blk % 2))[:, 1024:1536].rearrange("p (h c q) -> p h c q", h=2, c=2)
                        stv = steps[:].rearrange("p (c q) -> p c q", c=2)
                        for hh in range(2):
                            stt('dve', tl[:, hh, c_lo:2, :], stv[:, c_lo:2, :], cscale[hh], LT[:, hh, c_lo:2, :],
                                ALU.mult, ALU.add, ['steps', 'ps%d' % bl], tlr)
                        act(PTt[:, :, c_lo:2, :], tl[:, :, c_lo:2, :], AF.Exp, tlr, tlr)
                        st8['PTt'] = PTt
                        st8['ptr'] = tlr
                        return st8

                    def att_back(st8):
                        blk, r, n = st8['blk'], st8['r'], st8['n']
                        PTt, ptr, has_prev = st8['PTt'], st8['ptr'], st8['has_prev']
                        bo = nb('D')
                        OP = bank(bo)[0:65, 0:256].rearrange("p (h q) -> p h q", h=2)
                        for hh in range(2):
                            if has_prev:
                                mm(OP[:, hh, :], st8['vprev'][:, hh, :], PTt[:, hh, 0, :], True, False,
                                   st8['pres'] + ptr, ['ps%d' % bo])
                            mm(OP[:, hh, :], V[:, blk, hh, :], PTt[:, hh, 1, :], not has_prev, True,
                               R(4, 2) + ptr, ['ps%d' % bo])
                        st_tok = n * 128 * d + r
                        av = acc[0:65, :, st_tok:st_tok + 127 * d + 1:d]
                        if g == 0:
                            cp('dve', av, OP, ['ps%d' % bo], R(12, 4))
                        else:
                            tt('dve', av, OP, av, ALU.add, ['ps%d' % bo] + R(12, 4), R(12, 4))

                    pend = None
                    for blk in range(0 if 'noAttn' in DBG else 16):
                        cur = att_front(blk)
                        if pend is not None:
                            att_back(pend)
                        pend = cur
                    if pend is not None:
                        att_back(pend)
                for hh in range(0 if 'noNorm' in DBG else 2):
                    for tg in range(4):
                        bq = nb('A')
                        mm(bank(bq)[0:64, :], sel65[:], acc[0:65, hh, tg * 512:(tg + 1) * 512], True, True,
                           ['sel65'] + R(12, 4), ['ps%d' % bq])
                        rc = A(1)[0:64, 512:1024]
                        S.op('dve', lambda e, o=rc, i=bank(bq)[0:64, :]: e.reciprocal(out=o, in_=i),
                             reads=['ps%d' % bq], writes=R(1))
                        tt('dve', oT[0:64, 2 * hp + hh, tg * 512:(tg + 1) * 512], acc[0:64, hh, tg * 512:(tg + 1) * 512],
                           rc, ALU.mult, R(12, 4) + R(1), R(16, 8))
            woc = A16(0, 4).rearrange("p (h e) -> p h e", h=8)
            ld(woc[0:64], w_out_c[li].rearrange("(h p) e -> p h e", p=64), R(0, 4))
            gt1, bt1, tab1 = ln_tables(ln1g[l:l + 1, :], ln1b[l:l + 1, :], 1024, 4)
            proj_res_ln1(T, l, lambda i: [oT[0:64, h, i * 128:(i + 1) * 128] for h in range(8)],
                         lambda j, half: woc[0:64, j, half * 512:(half + 1) * 512],
                         R(16, 8) + R(0, 4), gt1, bt1, tab1)

        def c_layer_sample(l):
            li = l // 2
            T = SM
            qtm = A(12, 2)[0:32, 0:1536]
            ktm = A(14, 2)[0:32, 0:1536]
            vtm = A(16, 2)[0:32, 0:1536]
            for which, (dst, dres, scl) in enumerate(((qtm, R(12, 2), 0.125), (ktm, R(14, 2), 1.0), (vtm, R(16, 2), 1.0))):
                for g in range(3):
                    wsl = A16(0, 2).rearrange("p (k e) -> p k e", k=8)
                    col = which * 1536 + g * 512
                    ld(wsl, w_qkv[li, :, col:col + 512].rearrange("(k p) e -> p k e", p=128), R(0, 2))
                    b = nb('A')
                    for kc in range(8):
                        mm(bank(b)[0:32, :], xT_s[:, kc, :], wsl[:, kc, :], kc == 0, kc == 7, ['xTs'] + R(0, 2),
                           ['ps%d' % b])
                    act(dst[:, g * 512:(g + 1) * 512], bank(b)[0:32, :], AF.Copy, ['ps%d' % b], dres, scale=scl)
            for g in range(3):
                L = WIN[g]
                for s4 in range(4):
                    st(o_kvs[g][li, s4, L - 8:L, 0:512], ktm[s4 * 8:(s4 + 1) * 8, g * 512:(g + 1) * 512], R(14, 2))
                    st(o_kvs[g][li, s4, L - 8:L, 512:1024], vtm[s4 * 8:(s4 + 1) * 8, g * 512:(g + 1) * 512], R(16, 2))
            st(qs_d, qtm, R(12, 2), ['qsd'])
            OA = [bank(6)[0:32, 0:260], bank(7)[0:32, 0:260]]
            kb16 = A16(18)[0:32, 0:1536]
            qb16 = A16(19)[0:32, 0:1536]
            act(kb16, ktm, AF.Copy, R(14, 2), R(18))
            act(qb16, qtm, AF.Copy, R(12, 2), R(19))
            vaug = A16(20)[0:32, 0:3 * 8 * 65].rearrange("p (g h e) -> p g h e", g=3, h=8)
            S.op('pool', lambda e: e.memset(vaug[:, :, :, 64:65], 1.0), reads=[], writes=R(20))
            cp('dve', vaug[:, :, :, 0:64], vtm.rearrange("p (g h e) -> p g h e", g=3, h=8), R(16, 2), R(20))
            for g in range(3):
                d = DIL[g]
                for h in range(8):
                    colq = g * 512 + h * 64
                    bt_ = nb('A')
                    tp(bank16(bt_)[0:64, 0:32], kb16[:, colq:colq + 64], ident[0:32, 0:32], R(18) + ['ident'], ['ps%d' % bt_])
                    tp(bank16(bt_)[0:64, 32:64], qb16[:, colq:colq + 64], ident[0:32, 0:32], R(19) + ['ident'], ['ps%d' % bt_])
                    kq = A16(21)[0:64, 0:64]
                    cp('dve', kq, bank16(bt_)[0:64, 0:64], ['ps%d' % bt_], R(21))
                    bl = nb('B')
                    mm(bank(bl)[0:32, 0:32], kq[:, 0:32], kq[:, 32:64], True, True, R(21), ['ps%d' % bl])
                    tl = A(22)[0:32, 0:32]
                    stt('dve', tl, nsteps[:, g, :], -SLOPES[h] * d, bank(bl)[0:32, 0:32], ALU.mult, ALU.add,
                        ['nsteps', 'ps%d' % bl], R(22))
                    pts = A16(23)[0:32, (g * 8 + h) * 32:(g * 8 + h + 1) * 32]
                    act(pts, tl, AF.Exp, R(22), R(23))
            zl = A16(21)[0:8, 1024:1056]
            zr = A16(21)[0:8, 1100:1360]
            S.op('pool', lambda e: e.memset(A16(21)[0:8, 1024:1360], 0.0), reads=[], writes=['a21z'])
            for hb in range(2):
                mm(OA[hb], zl, zr, True, False, ['a21z'], ['ps%d' % (6 + hb)])
            for g in range(3):
                for h in range(8):
                    hb = h // 4
                    pts = A16(23)[0:32, (g * 8 + h) * 32:(g * 8 + h + 1) * 32]
                    mm(OA[hb][:, (h % 4) * 65:(h % 4 + 1) * 65], pts, vaug[:, g, h, :], False, False,
                       R(23) + R(20), ['ps%d' % (6 + hb)])
            S.op('pool', lambda e: e.memset(A(6, 2), 0.0), reads=[], writes=R(6, 2))
            S.op('pool', lambda e: e.memset(A(0, 2), 0.0), reads=[], writes=R(0, 2))
            kvring = [6, 7, 0, 1]
            LTall = A(2, 1)[:, 0:768].rearrange("p (a h) -> p a h", h=8)
            spend = [None]

            def s_back(sb_):
                pt, vres, idx, tok = sb_['pt'], sb_['vres'], sb_['idx'], sb_['tok']
                for hb in range(2):
                    bpv = nb('A')
                    mm(bank(bpv)[0:8, 0:260], pt, A16(9 + (idx % 2))[:, hb * 260:(hb + 1) * 260], True, True,
                       sb_['ptres'] + vres, ['ps%d' % bpv])
                    mk = A16(11)[0:8, hb * 260:(hb + 1) * 260]
                    tt('dve', mk, bank(bpv)[0:8, 0:260], hmask[:, hb * 260:(hb + 1) * 260], ALU.mult,
                       ['ps%d' % bpv, 'hmask'], R(11))
                    mm(OA[hb], sel[:, tok, :], mk, False, False, ['sel'] + R(11), ['ps%d' % (6 + hb)])

            for s4 in range(4):
                for t in range(8):
                    tok = s4 * 8 + t
                    qbc = A(4, 2)[:, 0:1536]
                    ld(qbc, qs_d[tok:tok + 1, :].partition_broadcast(128), R(4, 2), reads=['qsd'], q='sp')
                    for g in range(3):
                        d = DIL[g]
                        L = WIN[g]
                        base = L + t - 128 * d
                        nvalid = 128 - t // d
                        idx = (s4 * 8 + t) * 3 + g
                        kv = A(kvring[idx % 4], 1)
                        kvr = R(kvring[idx % 4])
                        ld(kv[0:nvalid, :], cks[g][li, s4, base:base + (nvalid - 1) * d + 1:d, :], kvr, q='sp')
                        pr = A(8)[:, 0:512]
                        tt('dve', pr, kv[:, 0:512], qbc[:, g * 512:(g + 1) * 512], ALU.mult, kvr + R(4, 2), R(8))
                        S.op('dve', lambda e, o=LTall[:, idx, :], i=pr.rearrange("p (h e) -> p h e", h=8):
                             e.tensor_reduce(out=o, in_=i, axis=mybir.AxisListType.X, op=ALU.add),
                             reads=R(8), writes=R(2))
                        tsl = 3 if idx % 2 == 0 else 19
                        tl = A(tsl)[:, 0:8]
                        tt('dve', tl, LTall[:, idx, :], sbias[:, t * 3 + g, :], ALU.add, R(2) + ['sbias'], R(tsl))
                        pt = A16(tsl)[:, 1024:1032]
                        act(pt, tl, AF.Exp, R(tsl), R(tsl))
                        vb = A16(9 + (idx % 2))[:, 0:520].rearrange("p (h e) -> p h e", h=8)
                        vres = R(9 + (idx % 2))
                        S.op('pool', lambda e, vb=vb: e.memset(vb[:, :, 64:65], 1.0), reads=[], writes=vres)
                        cp('pool', vb[:, :, 0:64], kv[:, 512:1024].rearrange("p (h e) -> p h e", h=8), kvr, vres)
                        cur = dict(pt=pt, ptres=R(tsl), vres=vres, idx=idx, tok=tok)
                        if spend[0] is not None:
                            s_back(spend[0])
                        spend[0] = cur
            s_back(spend[0])
            of = A(12)[0:32, 0:520]
            for hb in range(2):
                mm(OA[hb], zl, zr, False, True, ['a21z'], ['ps%d' % (6 + hb)])
                cp('dve', of[:, hb * 260:(hb + 1) * 260], OA[hb], ['ps%d' % (6 + hb)], R(12))
            ofv = of.rearrange("p (h e) -> p h e", h=8)
            rc = A(13)[0:32, 0:8]
            S.op('dve', lambda e: e.reciprocal(out=rc, in_=ofv[:, :, 64]), reads=R(12), writes=R(13))
            ob = A16(14)[0:32, 0:512]
            tt('dve', ob.rearrange("p (h e) -> p h e", h=8), ofv[:, :, 0:64],
               rc.unsqueeze(2).to_broadcast([32, 8, 64]), ALU.mult, R(12) + R(13), R(14))
            oTs = A16(15)[0:64, 0:256].rearrange("p (h t) -> p h t", h=8)
            bt_ = nb('A')
            for h in range(8):
                tp(bank16(bt_)[0:64, h * 32:(h + 1) * 32], ob[:, h * 64:(h + 1) * 64], ident[0:32, 0:32],
                   R(14) + ['ident'], ['ps%d' % bt_])
            cp('dve', oTs, bank16(bt_)[0:64, 0:256].rearrange("p (h t) -> p h t", h=8), ['ps%d' % bt_], R(15))
            woc = A16(0, 4).rearrange("p (h e) -> p h e", h=8)
            ld(woc[0:64], w_out_c[li].rearrange("(h p) e -> p h e", p=64), R(0, 4))
            gt1, bt1, tab1 = ln_tables(ln1g[l:l + 1, :], ln1b[l:l + 1, :], 1024, 4)
            proj_res_ln1(T, l, lambda i: [oTs[:, h, :] for h in range(8)],
                         lambda j, half: woc[0:64, j, half * 512:(half + 1) * 512],
                         R(15) + R(0, 4), gt1, bt1, tab1)

        def load_x(T, src):
            for i in range(T.ntile):
                ld(T.xr(i), src[i * T.P:(i + 1) * T.P, :], [T.xr_res(i)], q='sp')
                to_xT(T, i, 11)

        def store_y(T, dst):
            for i in range(T.ntile):
                st(dst[i * T.P:(i + 1) * T.P, :], T.xr(i), [T.xr_res(i)])

        for sg in range(nsg if do_prompt else 0):
            with_s = do_sample and sg == 0
            load_x(PR, xp[sg * NT:(sg + 1) * NT, :])
            if with_s:
                load_x(SM, xs)
            for l in range(nlayers):
                if l % 2 == 0:
                    if with_s:
                        ab_layer(SM, l, 0)
                    ab_layer(PR, l, sg)
                else:
                    if with_s:
                        c_layer_sample(l)
                    c_layer_prompt(l, sg)
                mlp([PR, SM] if with_s else [PR], l)
            store_y(PR, yp[sg * NT:(sg + 1) * NT, :])
            if with_s:
                store_y(SM, ys)
        if do_sample and not do_prompt:
            load_x(SM, xs)
            for l in range(nlayers):
                if l % 2 == 0:
                    ab_layer(SM, l, 0)
                else:
                    c_layer_sample(l)
                mlp([SM], l)
            store_y(SM, ys)
        S.finish()
        S.emit()
    return nc


def _consts():
    c = {}
    c["c_ident"] = np.eye(128, dtype=np.float32)
    p = np.arange(128)[:, None]
    q = np.arange(128)[None, :]
    sp = np.where(q <= p, 128 + q - p, BIG).astype(np.float32)
    so = np.where(q >= p, q - p, BIG).astype(np.float32)
    c["c_steps"] = np.concatenate([sp, so], axis=1).astype(np.float32)
    c["c_tril"] = (p <= q).astype(np.float32)
    k = np.arange(32)[:, None]
    qq = np.arange(32)[None, :]
    same = (k // 8) == (qq // 8)
    c["c_bd32"] = (same & ((k % 8) <= (qq % 8))).astype(np.float32)
    sb = np.zeros((128, 24, 8), np.float32)
    for t in range(8):
        for g in range(3):
            d = DIL[g]
            nvalid = 128 - t // d
            for h in range(8):
                v = -SLOPES[h] * d * (128.0 - np.arange(128))
                v[nvalid:] = -BIG
                sb[:, t * 3 + g, h] = v
    c["c_sbias"] = sb.reshape(128, 192)
    ns = np.full((32, 3, 32), BIG, np.float32)
    for kk in range(32):
        for q_ in range(32):
            if kk // 8 != q_ // 8:
                continue
            dt = (q_ % 8) - (kk % 8)
            for g in range(3):
                if dt >= 0 and dt % DIL[g] == 0:
                    ns[kk, g, q_] = dt // DIL[g]
    c["c_nsteps"] = ns.reshape(32, 96)
    hm = np.zeros((8, 8, 65), np.float32)
    for h in range(8):
        hm[h, h, :] = 1.0
    c["c_hmask"] = hm.reshape(8, 520)
    se = np.zeros((8, 32, 32), np.float32)
    for tok in range(32):
        se[:, tok, tok] = 1.0
    c["c_sel"] = se.reshape(8, 1024)
    s65 = np.zeros((65, 64), np.float32)
    s65[64, :] = 1.0
    c["c_sel65"] = s65
    return c


_NC_CACHE = {}


def kernel(x_prompt, x_sample, state_conv, cache_kv_w128, cache_kv_w512, cache_kv_w2048,
           w_in_ab, ln_v_g, ln_v_b, w_spatial, b_spatial, conv_w, w_out_ab,
           w_qkv_c, w_out_c, ln1_g, ln1_b, ln2_g, ln2_b, w_mlp_up, w_mlp_down):
    f = lambda a: np.ascontiguousarray(np.asarray(a, dtype=np.float32))
    if "nc" not in _NC_CACHE:
        _NC_CACHE["nc"] = build_program()
    nc = _NC_CACHE["nc"]
    consts = _consts()
    shared = {
        "w_in_ab": f(w_in_ab), "ln_v_g": f(ln_v_g), "ln_v_b": f(ln_v_b), "w_spatial": f(w_spatial),
        "b_spatial": f(b_spatial), "conv_w": f(conv_w), "w_out_ab": f(w_out_ab), "w_qkv_c": f(w_qkv_c),
        "w_out_c": f(w_out_c), "ln1_g": f(ln1_g), "ln1_b": f(ln1_b), "ln2_g": f(ln2_g), "ln2_b": f(ln2_b),
        "w_mlp_up": f(w_mlp_up), "w_mlp_down": f(w_mlp_down),
    }
    shared.update(consts)
    xpn = f(x_prompt)
    xsn = f(x_sample).reshape(256, D)
    sc = f(state_conv)
    c128 = f(cache_kv_w128).reshape(2, 32, 128, 1024)
    c512 = f(cache_kv_w512).reshape(2, 32, 512, 1024)
    c2048 = f(cache_kv_w2048).reshape(2, 32, 2048, 1024)
    in_maps = []
    for c in range(NCORES):
        m = dict(shared)
        m["xp"] = xpn[c % 2]
        m["xs"] = np.ascontiguousarray(xsn[c * 32:(c + 1) * 32])
        m["sconv"] = np.ascontiguousarray(sc[:, c * 4:(c + 1) * 4])
        m["ck128"] = np.ascontiguousarray(c128[:, c * 4:(c + 1) * 4])
        m["ck512"] = np.ascontiguousarray(c512[:, c * 4:(c + 1) * 4])
        m["ck2048"] = np.ascontiguousarray(c2048[:, c * 4:(c + 1) * 4])
        in_maps.append(m)
    res = run_bass_kernel_spmd(nc, in_maps, core_ids=list(range(NCORES)))
    r = res.results
    y_prompt = np.stack([r[0]["yp"], r[1]["yp"]]).astype(np.float32)
    y_sample = np.concatenate([r[c]["ys"] for c in range(NCORES)], 0).reshape(32, 8, D).astype(np.float32)
    conv_p = np.stack([r[0]["o_convp"], r[1]["o_convp"]], axis=1).astype(np.float32)
    conv_s = np.concatenate([r[c]["o_convs"] for c in range(NCORES)], axis=1).astype(np.float32)
    chunk_v = np.concatenate([r[c]["o_chunkv"] for c in range(NCORES)], axis=1).astype(np.float32)
    outs = [y_prompt, y_sample, conv_p, conv_s, chunk_v]
    for nm, L in (("o_kvp128", 128), ("o_kvp512", 512), ("o_kvp2048", 2048)):
        a = np.stack([r[0][nm], r[1][nm]], axis=1).astype(np.float32)
        outs.append(a.reshape(2, 2, L, 2, 8, 64))
    for nm, L in (("o_kvs128", 128), ("o_kvs512", 512), ("o_kvs2048", 2048)):
        a = np.concatenate([r[c][nm] for c in range(NCORES)], axis=1).astype(np.float32)
        outs.append(a.reshape(2, 32, L, 2, 8, 64))
    return tuple(outs)
```
